# Optimizing a Trainium2 kernel written in Bass

```python
import math
import jax
import jax.numpy as jnp
from jax import lax
import numpy as np

D_MODEL = 2048
BATCH = 32
SEQ = 256
DEPTH = 4
DEC_BATCH = 2
DEC_SEQ = 2048
PAST_LEN = 256

GRID_W = 64
F32 = jnp.float32
RMS_EPS = 1e-6
H_A = 8
DK_A = 64
DV_A = 128
MLSTM_CHUNK = 64
H_B = 8
DB = 64
DVB = 2 * DB
ATTN_Q_BLOCK = 128
ROPE_THETA = 10000.0
ROPE_PAIRS = DB // 4
D_INNER = 2 * D_MODEL
P_C = 64
H_C = D_INNER // P_C
G_C = 8
R_C = H_C // G_C
N_C = 128
SSD_CONV = 4
CONV_PAD_L = SSD_CONV // 2
CONV_PAD_R = SSD_CONV - 1 - SSD_CONV // 2
SSD_CHUNK = 128
DT_MIN = 0.001
DT_MAX = 0.1
D_FF = -(-8 * D_MODEL // (3 * 256)) * 256
N_EVEN = (DEPTH + 1) // 2
N_ODD = DEPTH // 2
A_QK = H_A * DK_A
A_V = H_A * DV_A
B_QK = H_B * 2 * DB
B_V = H_B * DVB
E_IN = 2 * A_QK + 2 * A_V + 4 * H_A + 2 * B_QK + B_V
CONV_CH = D_INNER + 2 * G_C * N_C
O_IN = D_INNER + CONV_CH + 2 * H_C

kernel_name = 'hybrid_mlstm_diffattn_ssd_prefix_denoise_step'


def _split(x, sizes):
    idx, acc = [], 0
    for s in sizes[:-1]:
        acc += s
        idx.append(acc)
    return jnp.split(x, idx, axis=-1)


def rms_norm(x, gain):
    xf = x.astype(F32)
    y = xf * lax.rsqrt(jnp.mean(xf * xf, axis=-1, keepdims=True) + RMS_EPS)
    return (y * gain.astype(F32)).astype(x.dtype)


def axial_rope_tables(n_tokens):
    n_rows = n_tokens // GRID_W
    rows, cols = jnp.meshgrid(jnp.arange(n_rows, dtype=F32), jnp.arange(GRID_W, dtype=F32), indexing='ij')
    inv_freq = ROPE_THETA ** (-jnp.arange(ROPE_PAIRS, dtype=F32) / ROPE_PAIRS)
    ang = jnp.stack([rows.reshape(-1, 1) * inv_freq, cols.reshape(-1, 1) * inv_freq], axis=0)
    return jnp.cos(ang), jnp.sin(ang)


def apply_axial_rope(x, cos, sin):
    xf = x.astype(F32)
    parts = []
    for axis in range(2):
        u = xf[..., axis * (DB // 2):(axis + 1) * (DB // 2)]
        u1, u2 = u[..., :ROPE_PAIRS], u[..., ROPE_PAIRS:]
        c_ = cos[axis][None, :, None, None, :]
        s_ = sin[axis][None, :, None, None, :]
        parts += [u1 * c_ - u2 * s_, u2 * c_ + u1 * s_]
    return jnp.concatenate(parts, axis=-1).astype(x.dtype)


def mlstm_scan(q, k, v, li, lf, state):
    b_, h_, L = q.shape[:3]
    nc = L // MLSTM_CHUNK

    def to_chunks(t):
        return jnp.moveaxis(t.reshape(b_, h_, nc, MLSTM_CHUNK, *t.shape[3:]), 2, 0)

    causal = jnp.tril(jnp.ones((MLSTM_CHUNK, MLSTM_CHUNK), dtype=bool))

    def step(carry, xs):
        C, n, m = carry
        qj, kj, vj, lij, lfj = xs
        b = jnp.cumsum(lfj, axis=-1)
        dmat = jnp.where(causal, b[..., :, None] - b[..., None, :] + lij[..., None, :], -jnp.inf)
        inter = b + m[..., None]
        m_row = jnp.maximum(inter, jnp.max(dmat, axis=-1))
        w_inter = jnp.exp(inter - m_row)
        s = jnp.einsum('bhid,bhjd->bhij', qj, kj) * jnp.exp(dmat - m_row[..., None])
        num = w_inter[..., None] * jnp.einsum('bhvd,bhid->bhiv', C, qj) + jnp.einsum('bhij,bhjv->bhiv', s, vj)
        den = w_inter * jnp.einsum('bhd,bhid->bhi', n, qj) + jnp.sum(s, axis=-1)
        h = num / jnp.maximum(jnp.abs(den), jnp.exp(-m_row))[..., None]
        b_end = b[..., -1]
        g = b_end[..., None] - b + lij
        m_new = jnp.maximum(b_end + m, jnp.max(g, axis=-1))
        decay = jnp.exp(b_end + m - m_new)
        wk = jnp.exp(g - m_new[..., None])[..., None] * kj
        C = decay[..., None, None] * C + jnp.einsum('bhjv,bhjd->bhvd', vj, wk)
        n = decay[..., None] * n + jnp.sum(wk, axis=2)
        return (C, n, m_new), h

    state = tuple(s.astype(F32) for s in state)
    final, hs = lax.scan(step, state, tuple(to_chunks(t) for t in (q, k, v, li, lf)))
    return jnp.moveaxis(hs, 0, 2).reshape(b_, h_, L, v.shape[-1]), final


def ssd_scan(x, dt, a, bm, cm, s0):
    b_, L = x.shape[:2]
    nc = L // SSD_CHUNK

    def to_chunks(t):
        return jnp.moveaxis(t.reshape(b_, nc, SSD_CHUNK, *t.shape[2:]), 1, 0)

    causal = jnp.tril(jnp.ones((SSD_CHUNK, SSD_CHUNK), dtype=bool))[None, :, :, None, None]

    def step(S, xs):
        xc, dtc, bc, cc = xs
        acum = jnp.cumsum(dtc * a, axis=1)
        seg = jnp.where(causal, acum[:, :, None] - acum[:, None, :], -jnp.inf)
        cb = jnp.einsum('bign,bjgn->bijg', cc, bc)
        w = cb[..., None] * jnp.exp(seg) * dtc[:, None]
        y = jnp.einsum('bijgr,bjgrp->bigrp', w, xc)
        y = y + jnp.einsum('bign,bgrpn->bigrp', cc, S) * jnp.exp(acum)[..., None]
        to_end = (jnp.exp(acum[:, -1:] - acum) * dtc)[..., None] * xc
        S = S * jnp.exp(acum[:, -1])[..., None, None] + jnp.einsum('bjgrp,bjgn->bgrpn', to_end, bc)
        return S, y

    S, ys = lax.scan(step, s0.astype(F32), tuple(to_chunks(t) for t in (x, dt, bm, cm)))
    return jnp.moveaxis(ys, 0, 1).reshape(x.shape), S


def diff_attend(q, k, v, lam):
    b_, lq = q.shape[:2]
    nb = lq // ATTN_Q_BLOCK
    qb = jnp.moveaxis(q.astype(F32).reshape(b_, nb, ATTN_Q_BLOCK, H_B, 2, DB), 1, 0)
    kf = k.astype(F32)
    vf = v.astype(F32)
    scale = DB ** -0.5

    def one_block(qblk):
        p = jax.nn.softmax(jnp.einsum('bqhcd,bkhcd->bchqk', qblk, kf) * scale, axis=-1)
        w = p[:, 0] - lam * p[:, 1]
        return jnp.einsum('bhqk,bkhv->bqhv', w, vf)

    out = lax.map(one_block, qb)
    return jnp.moveaxis(out, 0, 1).reshape(b_, lq, H_B, DVB)


def even_mixer(h, P, e, layer_idx, ctx):
    b_, L, _ = h.shape
    proj = jnp.einsum('bld,de->ble', h, P['w_in_even'][e])
    aq, ak, av, ao, ag, bq, bk, bv = _split(proj, [A_QK, A_QK, A_V, A_V, 4 * H_A, B_QK, B_QK, B_V])

    def heads(t, d):
        return t.astype(F32).reshape(b_, L, H_A, d).transpose(0, 2, 1, 3)

    q = heads(aq, DK_A)
    k = heads(ak, DK_A) * (DK_A ** -0.5)
    v = heads(av, DV_A)
    g = (ag.astype(F32) + P['b_gate_mlstm'][e].astype(F32)).reshape(b_, L, 4, H_A).transpose(2, 0, 3, 1)
    if ctx is None:
        zero = (jnp.zeros((b_, H_A, DV_A, DK_A), F32), jnp.zeros((b_, H_A, DK_A), F32), jnp.zeros((b_, H_A), F32))
        init_f, init_b = zero, zero
    else:
        init_f, init_b = ctx['mlstm_f'], ctx['mlstm_b']
    rev = lambda t: jnp.flip(t, axis=2)
    h_f, st_f = mlstm_scan(q, k, v, g[0], jax.nn.log_sigmoid(g[2]), init_f)
    h_b, st_b = mlstm_scan(rev(q), rev(k), rev(v), rev(g[1]), rev(jax.nn.log_sigmoid(g[3])), init_b)
    h_a = (h_f + rev(h_b)).transpose(0, 2, 1, 3)
    h_a = rms_norm(h_a, P['mlstm_norm'][e].reshape(H_A, DV_A)).reshape(b_, L, A_V)
    h_a = h_a * jax.nn.sigmoid(ao.astype(F32))

    qd = rms_norm(bq.reshape(b_, L, H_B, 2, DB), P['q_norm'][e])
    kd = rms_norm(bk.reshape(b_, L, H_B, 2, DB), P['k_norm'][e])
    vd = bv.reshape(b_, L, H_B, DVB)
    if ctx is None:
        k_all, v_all = kd, vd
    else:
        cos, sin = axial_rope_tables(L)
        qd = apply_axial_rope(qd, cos, sin)
        k_all = jnp.concatenate([ctx['k'].astype(kd.dtype), apply_axial_rope(kd, cos, sin)], axis=1)
        v_all = jnp.concatenate([ctx['v'].astype(vd.dtype), vd], axis=1)
    lam_init = 0.8 - 0.6 * math.exp(-0.3 * layer_idx)
    lam = (jnp.exp(jnp.sum(P['lambda_q1'][e] * P['lambda_k1'][e]))
           - jnp.exp(jnp.sum(P['lambda_q2'][e] * P['lambda_k2'][e])) + lam_init).astype(F32)
    o = diff_attend(qd, k_all, v_all, lam)
    o = rms_norm(o, P['diff_norm'][e]) * (1.0 - lam_init)

    cat = jnp.concatenate([h_a, o.reshape(b_, L, B_V)], axis=-1).astype(h.dtype)
    out = jnp.einsum('ble,ed->bld', cat, P['w_out_even'][e])
    if ctx is not None:
        return out, None
    new = {'k': kd, 'v': vd,
           'c': jnp.stack([st_f[0], st_b[0]], axis=1),
           'n': jnp.stack([st_f[1], st_b[1]], axis=1),
           'm': jnp.stack([st_f[2], st_b[2]], axis=1)}
    return out, new


def odd_mixer(h, P, o, ctx):
    b_, L, _ = h.shape
    proj = jnp.einsum('bld,de->ble', h, P['w_in_odd'][o])
    z, xbc, dtr = _split(proj, [D_INNER, CONV_CH, 2 * H_C])
    wconv = P['conv_w'][o].astype(xbc.dtype)[:, None, :]
    xbc = lax.conv_general_dilated(xbc, wconv, window_strides=(1,), padding=[(CONV_PAD_L, CONV_PAD_R)],
                                   dimension_numbers=('NWC', 'WIO', 'NWC'), feature_group_count=CONV_CH)
    xbc = jax.nn.silu(xbc.astype(F32) + P['conv_b'][o].astype(F32))
    xs, bm, cm = _split(xbc, [D_INNER, G_C * N_C, G_C * N_C])
    xs = xs.reshape(b_, L, G_C, R_C, P_C)
    bm = bm.reshape(b_, L, G_C, N_C)
    cm = cm.reshape(b_, L, G_C, N_C)
    dt = jax.nn.softplus(dtr.astype(F32).reshape(b_, L, 2, H_C) + P['dt_bias'][o].astype(F32))
    dt = dt.reshape(b_, L, 2, G_C, R_C)
    A = -jnp.exp(P['a_log'][o].astype(F32)).reshape(2, G_C, R_C)
    if ctx is None:
        s0f = jnp.zeros((b_, G_C, R_C, P_C, N_C), F32)
        s0b = s0f
    else:
        s0f = ctx['ssd_f'].reshape(b_, G_C, R_C, P_C, N_C)
        s0b = ctx['ssd_b'].reshape(b_, G_C, R_C, P_C, N_C)
    rev = lambda t: jnp.flip(t, axis=1)
    y_f, s_f = ssd_scan(xs, dt[:, :, 0], A[0], bm, cm, s0f)
    y_b, s_b = ssd_scan(rev(xs), rev(dt[:, :, 1]), A[1], rev(bm), rev(cm), s0b)
    y = y_f + rev(y_b) + P['d_skip'][o].astype(F32).reshape(G_C, R_C, 1) * xs
    y = y.reshape(b_, L, D_INNER) * jax.nn.silu(z.astype(F32))
    y = rms_norm(y, P['ssd_norm'][o]).astype(h.dtype)
    out = jnp.einsum('ble,ed->bld', y, P['w_out_odd'][o])
    if ctx is not None:
        return out, None
    new = {'ssd': jnp.stack([s_f.reshape(b_, H_C, P_C, N_C), s_b.reshape(b_, H_C, P_C, N_C)], axis=1)}
    return out, new


def block(x, cvec, l, P, ctx):
    mod = (jnp.einsum('bd,de->be', jax.nn.silu(cvec), P['w_ada'][l]) + P['b_ada'][l])[:, None, :]
    sh1, sc1, g1, sh2, sc2, g2 = jnp.split(mod, 6, axis=-1)
    h = rms_norm(x, P['norm_mix'][l]) * (1 + sc1) + sh1
    if l % 2 == 0:
        m, new = even_mixer(h, P, l // 2, l, ctx)
    else:
        m, new = odd_mixer(h, P, l // 2, ctx)
    x = x + g1 * m
    h = rms_norm(x, P['norm_ffn'][l]) * (1 + sc2) + sh2
    ff = jax.nn.silu(jnp.einsum('bld,df->blf', h, P['w_gate'][l])) * jnp.einsum('bld,df->blf', h, P['w_up'][l])
    x = x + g2 * jnp.einsum('blf,fd->bld', ff, P['w_down'][l])
    return x, new


def setup_inputs(seed: int = 0) -> dict:
    key = jax.random.key(seed)
    keys = iter(jax.random.split(key, 48))

    def nrm(shape, scale=1.0):
        return jax.random.normal(next(keys), shape, F32) * scale

    def gain(shape):
        return 1.0 + nrm(shape, 0.02)

    u_dt = jax.random.uniform(next(keys), (N_ODD, 2, H_C), F32)
    dt0 = jnp.exp(u_dt * (math.log(DT_MAX) - math.log(DT_MIN)) + math.log(DT_MIN))
    a_init = jax.random.uniform(next(keys), (N_ODD, 2, H_C), F32, 1.0, 16.0)
    f_bias = jnp.tile(jnp.linspace(3.0, 6.0, H_A, dtype=F32), (N_EVEN, 2))
    return {
        'x_prompt': nrm((BATCH, SEQ, D_MODEL)),
        'x_sample': nrm((DEC_BATCH, DEC_SEQ, D_MODEL)),
        'cache_attn_k': nrm((DEC_BATCH, N_EVEN, PAST_LEN, H_B, 2, DB)),
        'cache_attn_v': nrm((DEC_BATCH, N_EVEN, PAST_LEN, H_B, DVB)),
        'state_mlstm_c': nrm((DEC_BATCH, N_EVEN, 2, H_A, DV_A, DK_A)),
        'state_mlstm_n': nrm((DEC_BATCH, N_EVEN, 2, H_A, DK_A)),
        'state_mlstm_m': nrm((DEC_BATCH, N_EVEN, 2, H_A)),
        'state_ssd': nrm((DEC_BATCH, N_ODD, 2, H_C, P_C, N_C), 0.1),
        'c': nrm((DEC_BATCH, D_MODEL)),
        'c_ctx': nrm((D_MODEL,)),
        'w_ada': nrm((DEPTH, D_MODEL, 6 * D_MODEL), 0.5 * D_MODEL ** -0.5),
        'b_ada': nrm((DEPTH, 6 * D_MODEL), 0.02),
        'norm_mix': gain((DEPTH, D_MODEL)),
        'norm_ffn': gain((DEPTH, D_MODEL)),
        'w_gate': nrm((DEPTH, D_MODEL, D_FF), D_MODEL ** -0.5),
        'w_up': nrm((DEPTH, D_MODEL, D_FF), D_MODEL ** -0.5),
        'w_down': nrm((DEPTH, D_FF, D_MODEL), D_FF ** -0.5),
        'w_in_even': nrm((N_EVEN, D_MODEL, E_IN), D_MODEL ** -0.5),
        'b_gate_mlstm': jnp.concatenate([nrm((N_EVEN, 2 * H_A), 0.1), f_bias + nrm((N_EVEN, 2 * H_A), 0.1)], axis=-1),
        'mlstm_norm': gain((N_EVEN, A_V)),
        'q_norm': gain((N_EVEN, DB)),
        'k_norm': gain((N_EVEN, DB)),
        'lambda_q1': nrm((N_EVEN, DB), 0.1),
        'lambda_k1': nrm((N_EVEN, DB), 0.1),
        'lambda_q2': nrm((N_EVEN, DB), 0.1),
        'lambda_k2': nrm((N_EVEN, DB), 0.1),
        'diff_norm': gain((N_EVEN, DVB)),
        'w_out_even': nrm((N_EVEN, A_V + B_V, D_MODEL), (A_V + B_V) ** -0.5),
        'w_in_odd': nrm((N_ODD, D_MODEL, O_IN), D_MODEL ** -0.5),
        'conv_w': nrm((N_ODD, SSD_CONV, CONV_CH), SSD_CONV ** -0.5),
        'conv_b': nrm((N_ODD, CONV_CH), 0.02),
        'dt_bias': dt0 + jnp.log(-jnp.expm1(-dt0)),
        'a_log': jnp.log(a_init),
        'd_skip': gain((N_ODD, H_C)),
        'ssd_norm': gain((N_ODD, D_INNER)),
        'w_out_odd': nrm((N_ODD, D_INNER, D_MODEL), D_INNER ** -0.5),
    }


def reference(x_prompt, x_sample, cache_attn_k, cache_attn_v, state_mlstm_c, state_mlstm_n, state_mlstm_m,
              state_ssd, c, c_ctx, w_ada, b_ada, norm_mix, norm_ffn, w_gate, w_up, w_down, w_in_even,
              b_gate_mlstm, mlstm_norm, q_norm, k_norm, lambda_q1, lambda_k1, lambda_q2, lambda_k2, diff_norm,
              w_out_even, w_in_odd, conv_w, conv_b, dt_bias, a_log, d_skip, ssd_norm, w_out_odd):
    P = {'w_ada': w_ada, 'b_ada': b_ada, 'norm_mix': norm_mix, 'norm_ffn': norm_ffn,
         'w_gate': w_gate, 'w_up': w_up, 'w_down': w_down,
         'w_in_even': w_in_even, 'b_gate_mlstm': b_gate_mlstm, 'mlstm_norm': mlstm_norm,
         'q_norm': q_norm, 'k_norm': k_norm, 'lambda_q1': lambda_q1, 'lambda_k1': lambda_k1,
         'lambda_q2': lambda_q2, 'lambda_k2': lambda_k2, 'diff_norm': diff_norm, 'w_out_even': w_out_even,
         'w_in_odd': w_in_odd, 'conv_w': conv_w, 'conv_b': conv_b, 'dt_bias': dt_bias, 'a_log': a_log,
         'd_skip': d_skip, 'ssd_norm': ssd_norm, 'w_out_odd': w_out_odd}

    y_prompt = x_prompt
    even_new, odd_new = [], []
    for l in range(DEPTH):
        y_prompt, new = block(y_prompt, c_ctx[None, :], l, P, None)
        if l % 2 == 0:
            even_new.append(new)
        else:
            odd_new.append(new)
    new_attn_k = jnp.stack([s['k'] for s in even_new], axis=1)
    new_attn_v = jnp.stack([s['v'] for s in even_new], axis=1)
    new_mlstm_c = jnp.stack([s['c'] for s in even_new], axis=1)
    new_mlstm_n = jnp.stack([s['n'] for s in even_new], axis=1)
    new_mlstm_m = jnp.stack([s['m'] for s in even_new], axis=1)
    new_ssd = jnp.stack([s['ssd'] for s in odd_new], axis=1)

    y_sample = x_sample
    for l in range(DEPTH):
        i = l // 2
        if l % 2 == 0:
            ctx = {'k': cache_attn_k[:, i], 'v': cache_attn_v[:, i],
                   'mlstm_f': (state_mlstm_c[:, i, 0], state_mlstm_n[:, i, 0], state_mlstm_m[:, i, 0]),
                   'mlstm_b': (state_mlstm_c[:, i, 1], state_mlstm_n[:, i, 1], state_mlstm_m[:, i, 1])}
        else:
            ctx = {'ssd_f': state_ssd[:, i, 0], 'ssd_b': state_ssd[:, i, 1]}
        y_sample, _ = block(y_sample, c, l, P, ctx)

    return (y_prompt, y_sample, new_attn_k, new_attn_v, new_mlstm_c, new_mlstm_n, new_mlstm_m, new_ssd)
```

```python
import contextlib
import math
import numpy as np
import ml_dtypes
import concourse.bass as bass
import concourse.mybir as mybir
from concourse.bass_utils import run_bass_kernel_spmd

F32 = mybir.dt.float32
BF16 = mybir.dt.bfloat16
AF = mybir.ActivationFunctionType
ALU = mybir.AluOpType
AX = mybir.AxisListType

D = 2048
DFF = 5632
EIN = 6176
OIN = 10368
DIN = 4096
CONVC = 6144
EPS = 1e-6
BIG = 30000.0
ENGS = ["pe", "act", "dve", "pool", "sp"]
NDS = 8
ODD_STOP = 0


class Buf:
    __slots__ = ("t", "w", "r", "excl")

    def __init__(self, t, excl=False):
        self.t = t
        self.w = None
        self.r = {}
        self.excl = excl

    def __getitem__(self, k):
        return self.t[k]


def bc_last(ap, n):
    return bass.AP(ap.tensor, ap.offset, [list(x) for x in ap.ap] + [[0, n]])


def bc_mid(ap, n):
    a = [list(x) for x in ap.ap]
    return bass.AP(ap.tensor, ap.offset, [a[0], [0, n]] + a[1:])


def pbc(t, off, n, parts=128):
    return bass.AP(t, off, [[0, parts], [1, n]])


class KB:
    def __init__(self, nc, es):
        self.nc = nc
        self.e = dict(pe=nc.tensor, act=nc.scalar, dve=nc.vector, pool=nc.gpsimd, sp=nc.sync)
        self.semh = {}
        for k in ENGS:
            self.semh[k] = es.enter_context(nc.semaphore("sem_" + k))
        self.cnt = {k: 0 for k in ENGS}
        self.waited = {}
        self.dqc = {}
        self.dqn = {}
        for q in ("sp", "act", "pool"):
            self.dqc[q] = [0] * NDS
            self.dqn[q] = 0
            for i in range(NDS):
                self.semh[("d", q, i)] = es.enter_context(nc.semaphore("semd_%s_%d" % (q, i)))
        self.pend = {k: ([], []) for k in ENGS}
        self.uid = 0

    def name(self, p):
        self.uid += 1
        return "%s_%d" % (p, self.uid)

    def _deps(self, r, w):
        deps = {}
        for b in r:
            if b.w is not None and deps.get(b.w[0], 0) < b.w[1]:
                deps[b.w[0]] = b.w[1]
            if b.excl:
                for k, v in b.r.items():
                    if deps.get(k, 0) < v:
                        deps[k] = v
        for b in w:
            if b.w is not None and deps.get(b.w[0], 0) < b.w[1]:
                deps[b.w[0]] = b.w[1]
            for k, v in b.r.items():
                if deps.get(k, 0) < v:
                    deps[k] = v
        return deps

    def _wait(self, eng, deps):
        for k, v in deps.items():
            if k == eng and eng == "pe":
                continue
            if self.waited.get((eng, k), 0) >= v:
                continue
            self.e[eng].wait_ge(self.semh[k], v)
            self.waited[(eng, k)] = v

    def op(self, eng, fn, r=(), w=(), inc=True):
        self._wait(eng, self._deps(r, w))
        ins = fn(self.e[eng])
        pr, pw = self.pend[eng]
        if not inc:
            pr.extend(r)
            pw.extend(w)
            return
        self.cnt[eng] += 1
        ins.then_inc(self.semh[eng], 1)
        v = self.cnt[eng]
        for b in list(w) + pw:
            b.w = (eng, v)
            b.r = {}
        for b in list(r) + pr:
            if b.r.get(eng, 0) < v:
                b.r[eng] = v
        self.pend[eng] = ([], [])

    def dma(self, q, out, in_, r=(), w=(), slow=False):
        self._wait(q, self._deps(r, w))
        i = self.dqn[q]
        self.dqn[q] = (i + 1) % NDS
        key = ("d", q, i)
        c = self.dqc[q][i]
        if c > 0 and self.waited.get((q, key), 0) < 16 * c:
            self.e[q].wait_ge(self.semh[key], 16 * c)
            self.waited[(q, key)] = 16 * c
        if slow:
            ins = self.e[q].dma_start(out=out, in_=in_, allow_slow_non_contiguous=True)
        else:
            ins = self.e[q].dma_start(out=out, in_=in_)
        ins.then_inc(self.semh[key], 16)
        self.dqc[q][i] = c + 1
        v = 16 * (c + 1)
        for b in w:
            b.w = (key, v)
            b.r = {}
        for b in r:
            b.r[key] = v

    def barrier(self):
        for k in ENGS:
            assert not self.pend[k][0] and not self.pend[k][1]
        evs = [(k, self.cnt[k]) for k in ENGS if self.cnt[k] > 0]
        for q in self.dqc:
            for i in range(NDS):
                if self.dqc[q][i] > 0:
                    evs.append((("d", q, i), 16 * self.dqc[q][i]))
        for eng in ENGS:
            for k, v in evs:
                if k == eng and eng == "pe":
                    continue
                if self.waited.get((eng, k), 0) >= v:
                    continue
                self.e[eng].wait_ge(self.semh[k], v)
                self.waited[(eng, k)] = v


class Phase:
    def __init__(self, kb):
        self.kb = kb
        self.es = contextlib.ExitStack()

    def __enter__(self):
        self.es.__enter__()
        return self

    def __exit__(self, *a):
        self.kb.barrier()
        return self.es.__exit__(*a)

    def sb(self, shape, dt=F32, name="t"):
        return Buf(self.es.enter_context(self.kb.nc.sbuf_tensor(self.kb.name(name), list(shape), dt)))

    def ps(self, shape, dt=F32, name="p"):
        return Buf(self.es.enter_context(self.kb.nc.psum_tensor(self.kb.name(name), list(shape), dt)), excl=True)


def build(NSEG, layers):
    NT = NSEG * 256
    NCH = NT // 128
    TT = min(512, NT)
    NTILE = NT // TT
    NTC = TT // 128
    NK = 256 + NT
    NKB = NK // 128
    nc = bass.Bass("TRN2", target_bir_lowering=False)
    I = {}
    O = {}

    def din(name, shape, dt=F32):
        I[name] = nc.dram_tensor(name, list(shape), dt, kind="ExternalInput")
        return I[name]

    def dout(name, shape, dt=F32):
        O[name] = nc.dram_tensor(name, list(shape), dt, kind="ExternalOutput")
        return O[name]

    def dscr(name, shape, dt=F32):
        return nc.dram_tensor(name, list(shape), dt, kind="Internal")

    x_in = din("x", [NT, D])
    cT_in = din("cT", [128, 16])
    flag_in = din("flag", [128, 1])
    cmask_in = din("cmask", [128, 3])
    identf_in = din("identf", [128, 128])
    identb_in = din("identb", [128, 128], BF16)
    trif_in = din("trif", [2, 128, 128])
    negm_in = din("negm", [2, 128, 128], BF16)
    cos_in = din("cosT", [NT, 32])
    sin_in = din("sinT", [NT, 32])
    qseg_in = din("qseg", [16, NT], BF16)
    kseg_in = din("kseg", [16, NK], BF16)
    ctxk_in = din("ctxk", [2, 256, 1024])
    ctxv_in = din("ctxv", [2, 256, 1024])
    initC_in = din("initC", [2, 2, 8, 128, 64])
    initn_in = din("initn", [2, 2, 8, 64])
    initm_in = din("initm", [2, 2, 8])
    inits_in = din("inits", [2, 2, 128, 4096])
    W = {}
    for nm, shp in WSHAPES:
        W[nm] = din(nm, shp)

    y_out = dout("y", [NT, D])
    kd_out = dout("kd", [2, NT, 1024])
    vd_out = dout("vd", [2, NT, 1024])
    mC_out = dout("mC", [2, NSEG, 2, 8, 128, 64])
    mn_out = dout("mn", [2, NSEG, 2, 8, 64])
    mm_out = dout("mm", [2, NSEG, 2, 8])
    ssd_out = dout("ssd", [2, NSEG, 2, 128, 4096])

    mod_s = dscr("mod_s", [4, 6 * D])
    proj_s = dscr("proj_s", [NT + 4, OIN])
    cat_s = dscr("cat_s", [NT, DIN])
    hf_s = dscr("hf_s", [NT, DIN])
    xbc_s = dscr("xbc_s", [NT, CONVC])
    dtda_s = dscr("dtda_s", [NT, 256])

    es = contextlib.ExitStack()
    with es:
        kb = KB(nc, es)
        op = kb.op
        dma = kb.dma

        G = Phase(kb)
        G.__enter__()
        identf = G.sb([128, 128], F32, "identf")
        identb = G.sb([128, 128], BF16, "identb")
        trif = G.sb([128, 2, 128], F32, "trif")
        negm = G.sb([128, 2, 128], BF16, "negm")
        onesf = G.sb([128, 128], F32, "onesf")
        flag = G.sb([128, 1], F32, "flag")
        cmask = G.sb([128, 3], F32, "cmask")
        dma("sp", identf[:], identf_in[:], w=[identf])
        dma("sp", identb[:], identb_in[:], w=[identb])
        dma("sp", trif[:], trif_in.ap().rearrange("d t i -> t d i"), w=[trif])
        dma("sp", negm[:], negm_in.ap().rearrange("d t i -> t d i"), w=[negm])
        dma("sp", flag[:], flag_in[:], w=[flag])
        dma("sp", cmask[:], cmask_in[:], w=[cmask])
        op("dve", lambda e: e.memset(onesf[:], 1.0), w=[onesf])
        trib = G.sb([128, 2, 128], BF16, "trib")
        onesb = G.sb([128, 128], BF16, "onesb")
        op("dve", lambda e: e.tensor_copy(out=trib[:], in_=trif[:]), r=[trif], w=[trib])
        op("dve", lambda e: e.memset(onesb[:], 1.0), w=[onesb])

        with Phase(kb) as P:
            cT = P.sb([128, 16], F32)
            sc = P.sb([128, 16], F32)
            dma("sp", cT[:], cT_in[:], w=[cT])
            op("act", lambda e: e.activation(out=sc[:], in_=cT[:], func=AF.Silu), r=[cT], w=[sc])
            wa = [P.sb([128, 16, 512], F32, "wa") for _ in range(2)]
            brow = P.sb([1, 6 * D], F32, "brow")
            mrow = P.sb([1, 6 * D], F32, "mrow")
            pa = [P.ps([128, 512], F32, "pa") for _ in range(2)]
            it = 0
            for l in layers:
                dma("sp", brow[:], W["b_ada"][l:l + 1, :], w=[brow])
                for eb in range(24):
                    wb = wa[it % 2]
                    ps = pa[it % 2]
                    it += 1
                    dma("sp" if eb % 2 == 0 else "act", wb[:],
                        W["w_ada"][l].rearrange("(k p) e -> p k e", p=128)[:, :, eb * 512:(eb + 1) * 512], w=[wb])
                    for k in range(16):
                        op("pe", lambda e, k=k, wb=wb, ps=ps: e.matmul(ps[0:1, :], lhsT=sc[:, k:k + 1], rhs=wb[:, k, :],
                                                                     start=(k == 0), stop=(k == 15)),
                           r=[sc, wb], w=[ps], inc=(k == 15))
                    op("dve", lambda e, ps=ps, eb=eb: e.tensor_tensor(out=mrow[:, eb * 512:(eb + 1) * 512], in0=ps[0:1, :],
                                                                    in1=brow[:, eb * 512:(eb + 1) * 512], op=ALU.add),
                       r=[ps, brow], w=[mrow])
                for j in (1, 4):
                    op("dve", lambda e, j=j: e.tensor_scalar_add(out=mrow[:, j * D:(j + 1) * D], in0=mrow[:, j * D:(j + 1) * D],
                                                                 scalar1=1.0), r=[mrow], w=[mrow])
                dma("sp", mod_s[l:l + 1, :], mrow[:], r=[mrow])
            zt = P.sb([4, OIN], F32, "zt")
            op("dve", lambda e: e.memset(zt[:], 0.0), w=[zt])
            dma("sp", proj_s[0:2, :], zt[0:2, :], r=[zt])
            dma("sp", proj_s[NT + 2:NT + 4, :], zt[2:4, :], r=[zt])

        def load_bc(P, t, off, n, name="bc", q="sp"):
            b = P.sb([128, n], F32, name)
            dma(q, b[:], pbc(t, off, n), w=[b])
            return b

        def rstd_inplace(ss, n):
            op("dve", lambda e: e.tensor_scalar(out=ss[:], in0=ss[:], scalar1=1.0 / n, scalar2=EPS, op0=ALU.mult, op1=ALU.add),
               r=[ss], w=[ss])
            op("act", lambda e: e.activation(out=ss[:], in_=ss[:], func=AF.Sqrt), r=[ss], w=[ss])
            op("dve", lambda e: e.reciprocal(out=ss[:], in_=ss[:]), r=[ss], w=[ss])

        tcnt = [0]

        def transpose_to(src, ncol, dstT, tok0, ptp):
            nkk = ncol // 128
            for k0 in range(0, nkk, 8):
                kn = min(8, nkk - k0)
                pt = ptp[tcnt[0] % len(ptp)]
                tcnt[0] += 1
                for k in range(kn):
                    op("pe", lambda e, k=k, pt=pt: e.transpose(pt[:, k, :], src[:, (k0 + k) * 128:(k0 + k + 1) * 128], identb[:]),
                       r=[src, identb], w=[pt], inc=(k == kn - 1))
                if tcnt[0] % 2 == 0:
                    op("act", lambda e, pt=pt: e.copy(out=dstT[:, k0:k0 + kn, tok0:tok0 + 128], in_=pt[:, 0:kn, :]), r=[pt], w=[dstT])
                else:
                    op("dve", lambda e, pt=pt: e.tensor_copy(out=dstT[:, k0:k0 + kn, tok0:tok0 + 128], in_=pt[:, 0:kn, :]), r=[pt], w=[dstT])

        lcnt = [0]
        wcnt = [0]

        def linear(actT, ntc, nk, Wap, E, wbufs, pls, epi, ebw=512):
            Wv = Wap.rearrange("(k p) e -> p k e", p=128)
            for e0 in range(0, E, ebw):
                ec = min(ebw, E - e0)
                wb = wbufs[wcnt[0] % len(wbufs)]
                wcnt[0] += 1
                dma("pool", wb[:, 0:nk, 0:ec], Wv[:, :, e0:e0 + ec], w=[wb])
                for tc in range(ntc):
                    ps = pls[lcnt[0] % len(pls)]
                    lcnt[0] += 1
                    for k in range(nk):
                        op("pe", lambda e, k=k, ps=ps, wb=wb, tc=tc: e.matmul(ps[:, 0:ec], lhsT=actT[:, k, tc * 128:(tc + 1) * 128],
                                                                             rhs=wb[:, k, 0:ec], start=(k == 0), stop=(k == nk - 1)),
                           r=[actT, wb], w=[ps], inc=(k == nk - 1))
                    epi(tc, e0, ec, ps)

        def make_norm(P, l, which, hT, tp):
            j = 0 if which == 0 else 3
            gm = load_bc(P, mod_s, l * 6 * D + (j + 1) * D, D, "gm")
            sh = load_bc(P, mod_s, l * 6 * D + j * D, D, "sh", q="act")
            gn = load_bc(P, W["norm_mix" if which == 0 else "norm_ffn"], l * D, D, "gn")
            op("dve", lambda e: e.tensor_tensor(out=gm[:], in0=gm[:], in1=gn[:], op=ALU.mult), r=[gm, gn], w=[gm])
            junk = P.sb([128, D], F32, "junk")
            tmp = P.sb([128, D], F32, "tmp")
            hb = [P.sb([128, D], BF16, "hb") for _ in range(2)]
            ss = [P.sb([128, 1], F32, "ss") for _ in range(2)]

            def run(tc, xbuf, xap):
                s_ = ss[tc % 2]
                h_ = hb[tc % 2]
                op("act", lambda e: e.activation(out=junk[:], in_=xap, func=AF.Square, accum_out=s_[:]), r=[xbuf], w=[junk, s_])
                rstd_inplace(s_, D)
                op("dve", lambda e: e.scalar_tensor_tensor(out=tmp[:], in0=xap, scalar=s_[:, 0:1], in1=gm[:], op0=ALU.mult, op1=ALU.mult),
                   r=[xbuf, s_, gm], w=[tmp])
                op("pool", lambda e: e.tensor_tensor(out=h_[:], in0=tmp[:], in1=sh[:], op=ALU.add), r=[tmp, sh], w=[h_])
                transpose_to(h_, D, hT, tc * 128, tp)
            return run
        def even_mixer(l, li):
            lam_init = 0.8 - 0.6 * math.exp(-0.3 * l)
            PM = Phase(kb)
            PM.__enter__()
            gtok = PM.sb([128, NCH, 2, 3, 8], F32, "gtok")
            with Phase(kb) as P:
                bg = load_bc(P, W["b_gate_mlstm"], li * 32, 32, "bg")
                gch = [P.sb([128, 32], F32, "gch") for _ in range(2)]
                e1 = P.sb([128, 16], F32, "e1")
                lfs = [P.sb([128, 16], F32, "lf") for _ in range(2)]
                bT = [P.sb([8, NCH, 128], F32, "bT") for _ in range(2)]
                aT = [P.sb([8, NCH, 128], F32, "aT") for _ in range(2)]
                bend = [P.sb([8, NCH], F32, "bend") for _ in range(2)]
                mx = [P.sb([8, NCH], F32, "mx") for _ in range(2)]
                Mall = [P.sb([8, NCH], F32, "Mall") for _ in range(2)]
                minall = [P.sb([8, NCH], F32, "minall") for _ in range(2)]
                moutall = [P.sb([8, NCH], F32, "moutall") for _ in range(2)]
                negM = [P.sb([8, NCH], F32, "negM") for _ in range(2)]
                biasw = [P.sb([8, NCH], F32, "biasw") for _ in range(2)]
                rr = [P.sb([8, NCH], F32, "rr") for _ in range(2)]
                mseg = [P.sb([8, NSEG], F32, "mseg") for _ in range(2)]
                mi = [P.sb([8, 1], F32, "mi") for _ in range(2)]
                dg = [P.sb([8, 8], F32, "dg") for _ in range(2)]
                pcs = [P.ps([128, 512], F32, "pcs") for _ in range(3)]
                n = 0
                for c in range(NCH):
                    g = gch[c % 2]
                    lf = lfs[c % 2]
                    dma("sp", g[:], proj_s[2 + c * 128:2 + (c + 1) * 128, 3072:3104], w=[g])
                    op("dve", lambda e: e.tensor_tensor(out=g[:], in0=g[:], in1=bg[:], op=ALU.add), r=[g, bg], w=[g])
                    op("act", lambda e: e.activation(out=e1[:], in_=g[:, 16:32], func=AF.Exp, scale=-1.0), r=[g], w=[e1])
                    op("dve", lambda e: e.tensor_scalar_add(out=e1[:], in0=e1[:], scalar1=1.0), r=[e1], w=[e1])
                    op("act", lambda e: e.activation(out=e1[:], in_=e1[:], func=AF.Ln), r=[e1], w=[e1])
                    op("dve", lambda e: e.tensor_scalar_mul(out=lf[:], in0=e1[:], scalar1=-1.0), r=[e1], w=[lf])
                    for d in range(2):
                        ps = pcs[n % 3]
                        n += 1
                        op("pe", lambda e: e.matmul(ps[0:8, 0:128], lhsT=lf[:, d * 8:(d + 1) * 8], rhs=trif[:, d, :], start=True, stop=True),
                           r=[lf, trif], w=[ps], inc=False)
                        op("pe", lambda e: e.matmul(ps[0:8, 128:256], lhsT=g[:, d * 8:(d + 1) * 8], rhs=identf[:], start=True, stop=True),
                           r=[g, identf], w=[ps], inc=False)
                        op("pe", lambda e: e.matmul(ps[0:8, 256:257], lhsT=lf[:, d * 8:(d + 1) * 8], rhs=onesf[:, 0:1], start=True, stop=True),
                           r=[lf, onesf], w=[ps])
                        op("act", lambda e: e.copy(out=bT[d][:, c, :], in_=ps[0:8, 0:128]), r=[ps], w=[bT[d]])
                        op("dve", lambda e: e.tensor_tensor(out=aT[d][:, c, :], in0=ps[0:8, 128:256], in1=bT[d][:, c, :], op=ALU.subtract),
                           r=[ps, bT[d]], w=[aT[d]])
                        op("dve", lambda e: e.tensor_copy(out=bend[d][:, c:c + 1], in_=ps[0:8, 256:257]), r=[ps], w=[bend[d]])
                        op("dve", lambda e: e.tensor_reduce(out=mx[d][:, c:c + 1], in_=aT[d][:, c, :], axis=AX.X, op=ALU.max),
                           r=[aT[d]], w=[mx[d]])
                for d in range(2):
                    order = list(range(NCH)) if d == 0 else list(range(NCH - 1, -1, -1))
                    dma("sp", mi[d][:], bass.AP(initm_in, (li * 2 + d) * 8, [[1, 8], [1, 1]]), w=[mi[d]])
                    prev = None
                    for idx, c in enumerate(order):
                        segstart = (c % 2 == 0) if d == 0 else (c % 2 == 1)
                        segend = not segstart
                        if idx == 0:
                            op("dve", lambda e: e.tensor_copy(out=minall[d][:, c:c + 1], in_=mi[d][:]), r=[mi[d]], w=[minall[d]])
                        elif segstart:
                            op("dve", lambda e: e.tensor_scalar(out=minall[d][:, c:c + 1], in0=moutall[d][:, prev:prev + 1],
                                                                scalar1=flag[0:8, 0:1], scalar2=None, op0=ALU.mult),
                               r=[moutall[d], flag], w=[minall[d]])
                        else:
                            op("dve", lambda e: e.tensor_copy(out=minall[d][:, c:c + 1], in_=moutall[d][:, prev:prev + 1]),
                               r=[moutall[d]], w=[minall[d]])
                        op("dve", lambda e: e.tensor_tensor(out=Mall[d][:, c:c + 1], in0=minall[d][:, c:c + 1], in1=mx[d][:, c:c + 1], op=ALU.max),
                           r=[minall[d], mx[d]], w=[Mall[d]])
                        op("dve", lambda e: e.tensor_tensor(out=moutall[d][:, c:c + 1], in0=Mall[d][:, c:c + 1], in1=bend[d][:, c:c + 1], op=ALU.add),
                           r=[Mall[d], bend[d]], w=[moutall[d]])
                        if segend:
                            sg = c // 2
                            op("dve", lambda e: e.tensor_copy(out=mseg[d][:, sg:sg + 1], in_=moutall[d][:, c:c + 1]), r=[moutall[d]], w=[mseg[d]])
                        prev = c
                    op("dve", lambda e: e.tensor_scalar_mul(out=negM[d][:], in0=Mall[d][:], scalar1=-1.0), r=[Mall[d]], w=[negM[d]])
                    op("dve", lambda e: e.tensor_scalar_add(out=biasw[d][:], in0=negM[d][:], scalar1=math.log(0.125)), r=[negM[d]], w=[biasw[d]])
                    op("dve", lambda e: e.tensor_tensor(out=rr[d][:], in0=minall[d][:], in1=Mall[d][:], op=ALU.subtract),
                       r=[minall[d], Mall[d]], w=[rr[d]])
                    op("act", lambda e: e.activation(out=rr[d][:], in_=rr[d][:], func=AF.Exp), r=[rr[d]], w=[rr[d]])
                    dma("sp", bass.AP(mm_out, li * NSEG * 16 + d * 8, [[1, 8], [16, NSEG]]), mseg[d][:], r=[mseg[d]], slow=True)
                    for c in range(NCH):
                        op("act", lambda e: e.activation(out=aT[d][:, c, :], in_=aT[d][:, c, :], func=AF.Exp, bias=biasw[d][:, c:c + 1], scale=1.0),
                           r=[aT[d], biasw[d]], w=[aT[d]])
                        op("act", lambda e: e.activation(out=bT[d][:, c, :], in_=bT[d][:, c, :], func=AF.Exp, bias=negM[d][:, c:c + 1], scale=-1.0),
                           r=[bT[d], negM[d]], w=[bT[d]])
                        ps = pcs[n % 3]
                        n += 1
                        dgt = dg[c % 2]
                        op("dve", lambda e: e.tensor_scalar(out=dgt[:], in0=identf[0:8, 0:8], scalar1=rr[d][:, c:c + 1], scalar2=None, op0=ALU.mult),
                           r=[identf, rr[d]], w=[dgt])
                        op("pe", lambda e: e.transpose(ps[:, 0:8], aT[d][:, c, :], identf[0:8, 0:8]), r=[aT[d], identf], w=[ps], inc=False)
                        op("pe", lambda e: e.transpose(ps[:, 8:16], bT[d][:, c, :], identf[0:8, 0:8]), r=[bT[d], identf], w=[ps], inc=False)
                        op("pe", lambda e: e.matmul(ps[:, 16:24], lhsT=onesf[0:8, :], rhs=dgt[:], start=True, stop=True), r=[onesf, dgt], w=[ps])
                        op("dve", lambda e: e.tensor_copy(out=gtok[:, c, d, :, :], in_=ps[:, 0:24].rearrange("p (a h) -> p a h", a=3)),
                           r=[ps], w=[gtok])

            for d in range(2):
                with Phase(kb) as P:
                    CT = P.sb([64, 8, 129], F32, "CT")
                    CR = P.sb([64, 8, 129], F32, "CR")
                    CRb = P.sb([64, 8, 129], BF16, "CRb")
                    qkv = [P.sb([128, 2048], F32, "qkv") for _ in range(2)]
                    qb = P.sb([128, 512], BF16, "qb")
                    kbb = P.sb([128, 512], BF16, "kbb")
                    QT = P.sb([64, 8, 128], BF16, "QT")
                    KT = P.sb([64, 8, 128], BF16, "KT")
                    KW = P.sb([128, 8, 64], BF16, "KW")
                    VA = [P.sb([128, 8, 129], BF16, "VA") for _ in range(2)]
                    ST = [P.sb([128, 128], BF16, "ST") for _ in range(2)]
                    dn = [P.sb([128, 1], F32, "dn") for _ in range(2)]
                    hd = [P.sb([128, 1024], F32, "hd") for _ in range(2)]
                    ci = P.sb([128, 8, 64], F32, "ci")
                    co = P.sb([128, 8, 64], F32, "co")
                    psq = [P.ps([128, 8, 128], BF16, "psq") for _ in range(2)]
                    pss = [P.ps([128, 512], F32, "pss") for _ in range(2)]
                    psn = [P.ps([128, 512], F32, "psn") for _ in range(2)]
                    psu = P.ps([128, 512], F32, "psu")
                    pco = P.ps([128, 8, 64], F32, "pco")
                    if d == 1:
                        hfb = P.sb([128, 1024], F32, "hfb")
                        aob = P.sb([128, 1024], F32, "aob")
                        sqb = P.sb([128, 1024], F32, "sqb")
                        ssq = P.sb([128, 8], F32, "ssq")
                        mnb = load_bc(P, W["mlstm_norm"], li * 1024, 1024, "mnb")
                    for b in VA:
                        op("dve", lambda e, b=b: e.memset(b[:, :, 128:129], 1.0), w=[b])
                    dma("sp", ci[:], initC_in[li, d].rearrange("h v k -> v h k"), w=[ci])
                    for hh in range(0, 8, 4):
                        for j in range(4):
                            op("pe", lambda e, j=j: e.transpose(psn[0][0:64, j * 128:(j + 1) * 128], ci[:, hh + j, :], identf[:]),
                               r=[ci, identf], w=[psn[0]], inc=(j == 3))
                        op("dve", lambda e: e.tensor_copy(out=CT[:, hh:hh + 4, 0:128], in_=psn[0][0:64, 0:512].rearrange("p (h v) -> p h v", h=4)),
                           r=[psn[0]], w=[CT])
                    dma("sp", CT[:, :, 128:129], bass.AP(initn_in, (li * 2 + d) * 512, [[1, 64], [64, 8], [1, 1]]), w=[CT], slow=True)
                    order = list(range(NCH)) if d == 0 else list(range(NCH - 1, -1, -1))
                    for idx, c in enumerate(order):
                        segstart = (c % 2 == 0) if d == 0 else (c % 2 == 1)
                        segend = not segstart
                        sg = c // 2
                        buf = qkv[idx % 2]
                        va = VA[idx % 2]
                        hdb = hd[idx % 2]
                        r0 = 2 + c * 128
                        dma("sp", buf[:], proj_s[r0:r0 + 128, 0:2048], w=[buf])
                        if d == 1:
                            dma("act", hfb[:], hf_s[c * 128:(c + 1) * 128, 0:1024], w=[hfb])
                            dma("act", aob[:], proj_s[r0:r0 + 128, 2048:3072], w=[aob])
                        op("act", lambda e: e.copy(out=qb[:], in_=buf[:, 0:512]), r=[buf], w=[qb])
                        op("pool", lambda e: e.tensor_copy(out=kbb[:], in_=buf[:, 512:1024]), r=[buf], w=[kbb])
                        for h in range(8):
                            op("pe", lambda e, h=h: e.transpose(psq[0][0:64, h, :], qb[:, h * 64:(h + 1) * 64], identb[:]),
                               r=[qb, identb], w=[psq[0]], inc=(h == 7))
                        op("act", lambda e: e.copy(out=QT[:], in_=psq[0][0:64, :, :]), r=[psq[0]], w=[QT])
                        for h in range(8):
                            op("pe", lambda e, h=h: e.transpose(psq[1][0:64, h, :], kbb[:, h * 64:(h + 1) * 64], identb[:]),
                               r=[kbb, identb], w=[psq[1]], inc=(h == 7))
                        op("dve", lambda e: e.tensor_copy(out=KT[:], in_=psq[1][0:64, :, :]), r=[psq[1]], w=[KT])
                        op("dve", lambda e: e.tensor_tensor(out=KW[:], in0=buf[:, 512:1024].rearrange("p (h k) -> p h k", h=8),
                                                            in1=bc_last(gtok[:, c, d, 0, :], 64), op=ALU.mult), r=[buf, gtok], w=[KW])
                        op("act", lambda e: e.copy(out=va[:, :, 0:128], in_=buf[:, 1024:2048].rearrange("p (h v) -> p h v", h=8)), r=[buf], w=[va])
                        if idx > 0 and segstart:
                            op("dve", lambda e: e.tensor_scalar(out=CT[:], in0=CT[:], scalar1=flag[0:64, 0:1], scalar2=None, op0=ALU.mult),
                               r=[CT, flag], w=[CT])
                        op("dve", lambda e: e.tensor_tensor(out=CR[:], in0=CT[:], in1=bc_last(gtok[0:64, c, d, 2, :], 129), op=ALU.mult),
                           r=[CT, gtok], w=[CR])
                        op("pool", lambda e: e.tensor_copy(out=CRb[:], in_=CR[:]), r=[CR], w=[CRb])
                        def issue_S(h):
                            p_s = pss[h % 2]
                            op("pe", lambda e: e.matmul(p_s[:, 0:128], lhsT=KT[:, h, :], rhs=QT[:, h, :], start=True, stop=True),
                               r=[KT, QT], w=[p_s])
                        issue_S(0)
                        for h in range(8):
                            p_s = pss[h % 2]
                            p_n = psn[h % 2]
                            st = ST[h % 2]
                            dnn = dn[h % 2]
                            if h + 1 < 8:
                                issue_S(h + 1)
                            op("dve", lambda e: e.scalar_tensor_tensor(out=st[:], in0=p_s[:, 0:128], scalar=gtok[:, c, d, 0, h:h + 1],
                                                                       in1=trif[:, d, :], op0=ALU.mult, op1=ALU.mult),
                               r=[p_s, gtok, trif], w=[st])
                            op("pe", lambda e: e.matmul(p_n[:, 0:129], lhsT=st[:], rhs=va[:, h, :], start=True, stop=False),
                               r=[st, va], w=[p_n], inc=False)
                            op("pe", lambda e: e.matmul(p_n[:, 0:129], lhsT=QT[:, h, :], rhs=CRb[:, h, :], start=False, stop=True),
                               r=[QT, CRb], w=[p_n])
                            op("act", lambda e: e.activation(out=dnn[:], in_=p_n[:, 128:129], func=AF.Abs), r=[p_n], w=[dnn])
                            op("dve", lambda e: e.tensor_tensor(out=dnn[:], in0=dnn[:], in1=gtok[:, c, d, 1, h:h + 1], op=ALU.max), r=[dnn, gtok], w=[dnn])
                            op("dve", lambda e: e.reciprocal(out=dnn[:], in_=dnn[:]), r=[dnn], w=[dnn])
                            op("act", lambda e: e.activation(out=hdb[:, h * 128:(h + 1) * 128], in_=p_n[:, 0:128], func=AF.Copy, scale=dnn[:, 0:1]),
                               r=[p_n, dnn], w=[hdb])
                            op("pe", lambda e: e.matmul(psu[0:64, 0:129], lhsT=KW[:, h, :], rhs=va[:, h, :], start=True, stop=True),
                               r=[KW, va], w=[psu])
                            op("dve", lambda e: e.tensor_tensor(out=CT[:, h, :], in0=CR[:, h, :], in1=psu[0:64, 0:129], op=ALU.add),
                               r=[CR, psu], w=[CT])
                        if segend:
                            for h in range(8):
                                op("pe", lambda e, h=h: e.transpose(pco[:, h, :], CT[:, h, 0:128], identf[0:64, 0:64]),
                                   r=[CT, identf], w=[pco], inc=(h == 7))
                            op("act", lambda e: e.copy(out=co[:], in_=pco[:]), r=[pco], w=[co])
                            dma("sp", mC_out[li, sg, d].rearrange("h v k -> v h k"), co[:], r=[co])
                            dma("sp", bass.AP(mn_out, ((li * NSEG + sg) * 2 + d) * 512, [[1, 64], [64, 8], [1, 1]]), CT[:, :, 128:129],
                                r=[CT], slow=True)
                        if d == 0:
                            dma("sp", hf_s[c * 128:(c + 1) * 128, 0:1024], hdb[:], r=[hdb])
                        else:
                            op("dve", lambda e: e.tensor_tensor(out=hdb[:], in0=hdb[:], in1=hfb[:], op=ALU.add), r=[hdb, hfb], w=[hdb])
                            op("pool", lambda e: e.tensor_tensor(out=sqb[:], in0=hdb[:], in1=hdb[:], op=ALU.mult), r=[hdb], w=[sqb])
                            op("dve", lambda e: e.tensor_reduce(out=ssq[:], in_=sqb[:].rearrange("p (h v) -> p h v", h=8), axis=AX.X, op=ALU.add),
                               r=[sqb], w=[ssq])
                            rstd_inplace(ssq, 128)
                            op("dve", lambda e: e.tensor_tensor(out=hdb[:].rearrange("p (h v) -> p h v", h=8),
                                                                in0=hdb[:].rearrange("p (h v) -> p h v", h=8), in1=bc_last(ssq[:, :], 128), op=ALU.mult),
                               r=[hdb, ssq], w=[hdb])
                            op("pool", lambda e: e.tensor_tensor(out=hdb[:], in0=hdb[:], in1=mnb[:], op=ALU.mult), r=[hdb, mnb], w=[hdb])
                            op("act", lambda e: e.activation(out=aob[:], in_=aob[:], func=AF.Sigmoid), r=[aob], w=[aob])
                            op("dve", lambda e: e.tensor_tensor(out=hdb[:], in0=hdb[:], in1=aob[:], op=ALU.mult), r=[hdb, aob], w=[hdb])
                            dma("sp", cat_s[c * 128:(c + 1) * 128, 0:1024], hdb[:], r=[hdb])
            PM.__exit__(None, None, None)

            with Phase(kb) as P:
                QTa = P.sb([128, 8, NT], BF16, "QTa")
                KTa = P.sb([128, 8, NK], BF16, "KTa")
                VAa = P.sb([128, NKB, 8, 129], BF16, "VAa")
                qseg = P.sb([16, NT], BF16, "qseg")
                kseg = P.sb([16, NK], BF16, "kseg")
                dma("sp", qseg[:], qseg_in[:], w=[qseg])
                dma("sp", kseg[:], kseg_in[:], w=[kseg])
                lq = [load_bc(P, W[nm], li * 64, 64, nm) for nm in ("lambda_q1", "lambda_k1", "lambda_q2", "lambda_k2")]
                lsum = [P.sb([128, 1], F32, "lsum") for _ in range(2)]
                neglam = P.sb([128, 1], F32, "neglam")
                for j in range(2):
                    op("dve", lambda e, j=j: e.tensor_tensor(out=lq[2 * j][:], in0=lq[2 * j][:], in1=lq[2 * j + 1][:], op=ALU.mult),
                       r=[lq[2 * j], lq[2 * j + 1]], w=[lq[2 * j]])
                    op("dve", lambda e, j=j: e.tensor_reduce(out=lsum[j][:], in_=lq[2 * j][:], axis=AX.X, op=ALU.add), r=[lq[2 * j]], w=[lsum[j]])
                    op("act", lambda e, j=j: e.activation(out=lsum[j][:], in_=lsum[j][:], func=AF.Exp), r=[lsum[j]], w=[lsum[j]])
                op("dve", lambda e: e.tensor_tensor(out=neglam[:], in0=lsum[1][:], in1=lsum[0][:], op=ALU.subtract), r=lsum, w=[neglam])
                op("dve", lambda e: e.tensor_scalar_add(out=neglam[:], in0=neglam[:], scalar1=-lam_init), r=[neglam], w=[neglam])
                qnb = load_bc(P, W["q_norm"], li * 64, 64, "qnb")
                knb = load_bc(P, W["k_norm"], li * 64, 64, "knb")
                dnb = load_bc(P, W["diff_norm"], li * 128, 128, "dnb")
                op("dve", lambda e: e.tensor_scalar_mul(out=dnb[:], in0=dnb[:], scalar1=(1.0 - lam_init)), r=[dnb], w=[dnb])
                op("dve", lambda e: e.memset(VAa[:, :, :, 128:129], 1.0), w=[VAa])
                with Phase(kb) as P1:
                    raws = [P1.sb([128, 3072], F32, "raw") for _ in range(2)]
                    sqq = P1.sb([128, 2048], F32, "sqq")
                    ssq = P1.sb([128, 32], F32, "ssq")
                    qk = P1.sb([128, 2048], F32, "qk")
                    cst = [P1.sb([128, 32], F32, "cst") for _ in range(2)]
                    snt = [P1.sb([128, 32], F32, "snt") for _ in range(2)]
                    rot = P1.sb([128, 2048], BF16, "rot")
                    t1 = P1.sb([128, 1024], F32, "t1")
                    t2 = P1.sb([128, 1024], F32, "t2")
                    ptr = [P1.ps([128, 8, 128], BF16, "ptr") for _ in range(2)]
                    for c in range(NCH):
                        raw = raws[c % 2]
                        cs_ = cst[c % 2]
                        sn_ = snt[c % 2]
                        r0 = 2 + c * 128
                        dma("sp", raw[:], proj_s[r0:r0 + 128, 3104:6176], w=[raw])
                        dma("act", cs_[:], cos_in[c * 128:(c + 1) * 128, :], w=[cs_])
                        dma("act", sn_[:], sin_in[c * 128:(c + 1) * 128, :], w=[sn_])
                        op("pool", lambda e: e.tensor_tensor(out=sqq[:], in0=raw[:, 0:2048], in1=raw[:, 0:2048], op=ALU.mult), r=[raw], w=[sqq])
                        op("dve", lambda e: e.tensor_reduce(out=ssq[:], in_=sqq[:].rearrange("p (g k) -> p g k", g=32), axis=AX.X, op=ALU.add),
                           r=[sqq], w=[ssq])
                        rstd_inplace(ssq, 64)
                        op("dve", lambda e: e.tensor_tensor(out=qk[:].rearrange("p (g k) -> p g k", g=32),
                                                            in0=raw[:, 0:2048].rearrange("p (g k) -> p g k", g=32),
                                                            in1=bc_last(ssq[:, :], 64), op=ALU.mult), r=[raw, ssq], w=[qk])
                        op("dve", lambda e: e.tensor_tensor(out=qk[:, 0:1024].rearrange("p (g k) -> p g k", g=16),
                                                            in0=qk[:, 0:1024].rearrange("p (g k) -> p g k", g=16),
                                                            in1=bc_mid(qnb[:, :], 16), op=ALU.mult), r=[qk, qnb], w=[qk])
                        op("pool", lambda e: e.tensor_tensor(out=qk[:, 1024:2048].rearrange("p (g k) -> p g k", g=16),
                                                             in0=qk[:, 1024:2048].rearrange("p (g k) -> p g k", g=16),
                                                             in1=bc_mid(knb[:, :], 16), op=ALU.mult), r=[qk, knb], w=[qk])
                        dma("sp", kd_out[li, c * 128:(c + 1) * 128, :], qk[:, 1024:2048], r=[qk])
                        dma("sp", vd_out[li, c * 128:(c + 1) * 128, :], raw[:, 2048:3072], r=[raw])
                        x5 = qk[:, :].rearrange("p (g a t r) -> p g a t r", g=32, a=2, t=2, r=16)
                        r5 = rot[:, :].rearrange("p (g a t r) -> p g a t r", g=32, a=2, t=2, r=16)
                        u1, u2 = x5[:, :, :, 0, :], x5[:, :, :, 1, :]
                        o1, o2 = r5[:, :, :, 0, :], r5[:, :, :, 1, :]
                        cb_ = bc_mid(cs_[:, :].rearrange("p (a r) -> p a r", a=2), 32)
                        sb_ = bc_mid(sn_[:, :].rearrange("p (a r) -> p a r", a=2), 32)
                        t1v = t1[:, :].rearrange("p (g a r) -> p g a r", g=32, a=2)
                        t2v = t2[:, :].rearrange("p (g a r) -> p g a r", g=32, a=2)
                        op("dve", lambda e: e.tensor_tensor(out=t1v, in0=u1, in1=cb_, op=ALU.mult), r=[qk, cs_], w=[t1])
                        op("pool", lambda e: e.tensor_tensor(out=t2v, in0=u2, in1=sb_, op=ALU.mult), r=[qk, sn_], w=[t2])
                        op("dve", lambda e: e.tensor_tensor(out=o1, in0=t1v, in1=t2v, op=ALU.subtract), r=[t1, t2], w=[rot])
                        op("dve", lambda e: e.tensor_tensor(out=t1v, in0=u2, in1=cb_, op=ALU.mult), r=[qk, cs_], w=[t1])
                        op("pool", lambda e: e.tensor_tensor(out=t2v, in0=u1, in1=sb_, op=ALU.mult), r=[qk, sn_], w=[t2])
                        op("dve", lambda e: e.tensor_tensor(out=o2, in0=t1v, in1=t2v, op=ALU.add), r=[t1, t2], w=[rot])
                        for h in range(8):
                            op("pe", lambda e, h=h: e.transpose(ptr[0][:, h, :], rot[:, h * 128:(h + 1) * 128], identb[:]),
                               r=[rot, identb], w=[ptr[0]], inc=(h == 7))
                        op("act", lambda e: e.copy(out=QTa[:, :, c * 128:(c + 1) * 128], in_=ptr[0][:]), r=[ptr[0]], w=[QTa])
                        for h in range(8):
                            op("pe", lambda e, h=h: e.transpose(ptr[1][:, h, :], rot[:, 1024 + h * 128:1024 + (h + 1) * 128], identb[:]),
                               r=[rot, identb], w=[ptr[1]], inc=(h == 7))
                        op("dve", lambda e: e.tensor_copy(out=KTa[:, :, 256 + c * 128:256 + (c + 1) * 128], in_=ptr[1][:]), r=[ptr[1]], w=[KTa])
                        op("act", lambda e: e.copy(out=VAa[:, 2 + c, :, 0:128], in_=raw[:, 2048:3072].rearrange("p (h v) -> p h v", h=8)),
                           r=[raw], w=[VAa])
                    for cb2 in range(2):
                        raw = raws[cb2]
                        dma("sp", raw[:, 0:1024], ctxk_in[li, cb2 * 128:(cb2 + 1) * 128, :], w=[raw])
                        dma("act", raw[:, 2048:3072], ctxv_in[li, cb2 * 128:(cb2 + 1) * 128, :], w=[raw])
                        op("pool", lambda e: e.tensor_copy(out=rot[:, 0:1024], in_=raw[:, 0:1024]), r=[raw], w=[rot])
                        for h in range(8):
                            op("pe", lambda e, h=h: e.transpose(ptr[1][:, h, :], rot[:, h * 128:(h + 1) * 128], identb[:]),
                               r=[rot, identb], w=[ptr[1]], inc=(h == 7))
                        op("dve", lambda e: e.tensor_copy(out=KTa[:, :, cb2 * 128:(cb2 + 1) * 128], in_=ptr[1][:]), r=[ptr[1]], w=[KTa])
                        op("act", lambda e: e.copy(out=VAa[:, cb2, :, 0:128], in_=raw[:, 2048:3072].rearrange("p (h v) -> p h v", h=8)),
                           r=[raw], w=[VAa])
                with Phase(kb) as P2:
                    QB = TT
                    nqc = QB // 128
                    pS = [P2.ps([128, 512], F32, "pS") for _ in range(2)]
                    acc = [P2.ps([128, 512], F32, "acc") for _ in range(4)]
                    Eb = [P2.sb([128, 512], BF16, "Eb") for _ in range(2)]
                    o0 = P2.sb([128, 4, 128], F32, "o0")
                    od = P2.sb([128, 4, 1024], F32, "od")
                    rec = [P2.sb([128, 1], F32, "rec") for _ in range(4)]
                    tt = [P2.sb([128, 128], F32, "tt") for _ in range(2)]
                    sq2 = P2.sb([128, 1024], F32, "sq2")
                    ss2 = P2.sb([128, 8], F32, "ss2")
                    def issue_S(i, qbi, h, c2, kbi):
                        ps = pS[i % 2]
                        op("pe", lambda e: e.matmul(ps[:, 0:QB], lhsT=KTa[c2 * 64:(c2 + 1) * 64, h, kbi * 128:(kbi + 1) * 128],
                                                    rhs=QTa[c2 * 64:(c2 + 1) * 64, h, qbi * QB:(qbi + 1) * QB], start=True, stop=False),
                           r=[KTa, QTa], w=[ps], inc=False)
                        op("pe", lambda e: e.matmul(ps[:, 0:QB], lhsT=kseg[:, kbi * 128:(kbi + 1) * 128],
                                                    rhs=qseg[:, qbi * QB:(qbi + 1) * QB], start=False, stop=True),
                           r=[kseg, qseg], w=[ps])
                    for qbi in range(NT // QB):
                        seq = [(h, c2, kbi) for h in range(8) for c2 in range(2) for kbi in range(NKB)]
                        issue_S(0, qbi, *seq[0])
                        for i, (h, c2, kbi) in enumerate(seq):
                            ps = pS[i % 2]
                            eb = Eb[i % 2]
                            if i + 1 < len(seq):
                                issue_S(i + 1, qbi, *seq[i + 1])
                            op("act", lambda e: e.activation(out=eb[:, 0:QB], in_=ps[:, 0:QB], func=AF.Exp, scale=0.125), r=[ps], w=[eb])
                            for qc in range(nqc):
                                op("pe", lambda e, qc=qc: e.matmul(acc[qc][:, 0:129], lhsT=eb[:, qc * 128:(qc + 1) * 128], rhs=VAa[:, kbi, h, :],
                                                                   start=(kbi == 0), stop=(kbi == NKB - 1)),
                                   r=[eb, VAa], w=[acc[qc]], inc=(qc == nqc - 1))
                            if kbi == NKB - 1:
                                for qc in range(nqc):
                                    rc = rec[qc]
                                    op("dve", lambda e: e.reciprocal(out=rc[:], in_=acc[qc][:, 128:129]), r=[acc[qc]], w=[rc])
                                    if c2 == 0:
                                        op("act", lambda e: e.activation(out=o0[:, qc, :], in_=acc[qc][:, 0:128], func=AF.Copy, scale=rc[:, 0:1]),
                                           r=[acc[qc], rc], w=[o0])
                                    else:
                                        t_ = tt[qc % 2]
                                        op("act", lambda e: e.activation(out=t_[:], in_=acc[qc][:, 0:128], func=AF.Copy, scale=rc[:, 0:1]),
                                           r=[acc[qc], rc], w=[t_])
                                        op("dve", lambda e: e.scalar_tensor_tensor(out=od[:, qc, h * 128:(h + 1) * 128], in0=t_[:], scalar=neglam[:, 0:1],
                                                                                   in1=o0[:, qc, :], op0=ALU.mult, op1=ALU.add),
                                           r=[t_, neglam, o0], w=[od])
                        for qc in range(nqc):
                            op("pool", lambda e: e.tensor_tensor(out=sq2[:], in0=od[:, qc, :], in1=od[:, qc, :], op=ALU.mult), r=[od], w=[sq2])
                            op("dve", lambda e: e.tensor_reduce(out=ss2[:], in_=sq2[:].rearrange("p (h v) -> p h v", h=8), axis=AX.X, op=ALU.add),
                               r=[sq2], w=[ss2])
                            rstd_inplace(ss2, 128)
                            op("dve", lambda e: e.tensor_tensor(out=sq2[:].rearrange("p (h v) -> p h v", h=8),
                                                                in0=od[:, qc, :].rearrange("p (h v) -> p h v", h=8), in1=bc_last(ss2[:, :], 128), op=ALU.mult),
                               r=[od, ss2], w=[sq2])
                            op("pool", lambda e: e.tensor_tensor(out=sq2[:].rearrange("p (h v) -> p h v", h=8),
                                                                 in0=sq2[:].rearrange("p (h v) -> p h v", h=8), in1=bc_mid(dnb[:, :], 8), op=ALU.mult),
                               r=[sq2, dnb], w=[sq2])
                            t0 = qbi * QB + qc * 128
                            dma("sp", cat_s[t0:t0 + 128, 1024:2048], sq2[:], r=[sq2])
        def odd_mixer(l, li):
            for cbk in range(3):
                with Phase(kb) as P:
                    c0 = cbk * 2048
                    wk = P.sb([128, 4, 2048], F32, "wk")
                    for k in range(4):
                        dma("sp", wk[:, k, :], pbc(W["conv_w"], (li * 4 + k) * CONVC + c0, 2048), w=[wk])
                    bb = load_bc(P, W["conv_b"], li * CONVC + c0, 2048, "bb")
                    xk = [[P.sb([128, 2048], F32, "xk") for _ in range(4)] for _ in range(2)]
                    accs = [P.sb([128, 2048], F32, "acc") for _ in range(2)]
                    prs = [P.sb([128, 2048], F32, "pr") for _ in range(2)]
                    for c in range(NCH):
                        xs_ = xk[c % 2]
                        acc = accs[c % 2]
                        segstart = (c % 2 == 0)
                        for k in range(4):
                            dma("sp" if k % 2 == 0 else "act", xs_[k][:], proj_s[c * 128 + k:c * 128 + k + 128, 4096 + c0:4096 + c0 + 2048], w=[xs_[k]])

                        def tap(eng, out, k, mcol):
                            if mcol is None:
                                op(eng, lambda e: e.tensor_tensor(out=out[:], in0=xs_[k][:], in1=wk[:, k, :], op=ALU.mult), r=[xs_[k], wk], w=[out])
                            else:
                                op("dve", lambda e: e.scalar_tensor_tensor(out=out[:], in0=xs_[k][:], scalar=cmask[:, mcol:mcol + 1], in1=wk[:, k, :],
                                                                         op0=ALU.mult, op1=ALU.mult), r=[xs_[k], wk, cmask], w=[out])
                        tap("dve", acc, 0, 0 if segstart else None)
                        tap("pool", prs[0], 1, 1 if segstart else None)
                        op("dve", lambda e: e.tensor_tensor(out=acc[:], in0=acc[:], in1=prs[0][:], op=ALU.add), r=[acc, prs[0]], w=[acc])
                        tap("pool", prs[1], 2, None)
                        op("dve", lambda e: e.tensor_tensor(out=acc[:], in0=acc[:], in1=prs[1][:], op=ALU.add), r=[acc, prs[1]], w=[acc])
                        tap("pool", prs[0], 3, None if segstart else 2)
                        op("dve", lambda e: e.tensor_tensor(out=acc[:], in0=acc[:], in1=prs[0][:], op=ALU.add), r=[acc, prs[0]], w=[acc])
                        op("dve", lambda e: e.tensor_tensor(out=acc[:], in0=acc[:], in1=bb[:], op=ALU.add), r=[acc, bb], w=[acc])
                        op("act", lambda e: e.activation(out=acc[:], in_=acc[:], func=AF.Silu), r=[acc], w=[acc])
                        dma("sp", xbc_s[c * 128:(c + 1) * 128, c0:c0 + 2048], acc[:], r=[acc])
            if ODD_STOP == 1:
                return
            with Phase(kb) as P:
                dtb = load_bc(P, W["dt_bias"], li * 128, 128, "dtb")
                Ab = load_bc(P, W["a_log"], li * 128, 128, "Ab")
                op("act", lambda e: e.activation(out=Ab[:], in_=Ab[:], func=AF.Exp), r=[Ab], w=[Ab])
                op("dve", lambda e: e.tensor_scalar_mul(out=Ab[:], in0=Ab[:], scalar1=-1.0), r=[Ab], w=[Ab])
                dtr = [P.sb([128, 128], F32, "dtr") for _ in range(2)]
                ax = [P.sb([128, 128], F32, "ax") for _ in range(2)]
                dd = [P.sb([128, 256], F32, "dd") for _ in range(2)]
                for c in range(NCH):
                    t_, a_, d_ = dtr[c % 2], ax[c % 2], dd[c % 2]
                    dma("sp", t_[:], proj_s[2 + c * 128:2 + (c + 1) * 128, 10240:10368], w=[t_])
                    op("dve", lambda e: e.tensor_tensor(out=t_[:], in0=t_[:], in1=dtb[:], op=ALU.add), r=[t_, dtb], w=[t_])
                    op("act", lambda e: e.activation(out=a_[:], in_=t_[:], func=AF.Abs), r=[t_], w=[a_])
                    op("act", lambda e: e.activation(out=a_[:], in_=a_[:], func=AF.Exp, scale=-1.0), r=[a_], w=[a_])
                    op("dve", lambda e: e.tensor_scalar_add(out=a_[:], in0=a_[:], scalar1=1.0), r=[a_], w=[a_])
                    op("act", lambda e: e.activation(out=a_[:], in_=a_[:], func=AF.Ln), r=[a_], w=[a_])
                    op("dve", lambda e: e.tensor_scalar_max(out=t_[:], in0=t_[:], scalar1=0.0), r=[t_], w=[t_])
                    op("dve", lambda e: e.tensor_tensor(out=d_[:, 0:128], in0=t_[:], in1=a_[:], op=ALU.add), r=[t_, a_], w=[d_])
                    op("dve", lambda e: e.tensor_tensor(out=d_[:, 128:256], in0=d_[:, 0:128], in1=Ab[:], op=ALU.mult), r=[d_, Ab], w=[d_])
                    dma("sp", dtda_s[c * 128:(c + 1) * 128, :], d_[:], r=[d_])
            if ODD_STOP == 2:
                return
            for d in range(2):
                if ODD_STOP == 3 and d == 1:
                    return
                with Phase(kb) as P:
                    Sst = P.sb([128, 4096], F32, "Sst")
                    Sb = P.sb([128, 4096], BF16, "Sb")
                    xs = P.sb([128, 4096], F32, "xs")
                    bcm = P.sb([128, 2048], F32, "bcm")
                    dd = P.sb([128, 256], F32, "dd")
                    xsb = P.sb([128, 4096], BF16, "xsb")
                    xsc = P.sb([128, 4096], BF16, "xsc")
                    bcb = P.sb([128, 2048], BF16, "bcb")
                    BT = P.sb([128, 8, 128], BF16, "BT")
                    CTt = P.sb([128, 8, 128], BF16, "CTt")
                    cbT = P.sb([128, 8, 128], F32, "cbT")
                    yacc = P.sb([128, 4096], F32, "yacc")
                    acum = P.sb([128, 64], F32, "acum")
                    nacum = P.sb([128, 64], F32, "nacum")
                    acT = P.sb([64, 128], F32, "acT")
                    acTh = P.sb([64, 128], BF16, "acTh")
                    acThf = P.sb([64, 128], F32, "acThf")
                    acTl = P.sb([64, 128], BF16, "acTl")
                    dah = P.sb([128, 64], BF16, "dah")
                    dahf = P.sb([128, 64], F32, "dahf")
                    dal = P.sb([128, 64], BF16, "dal")
                    aend = P.sb([128, 64], F32, "aend")
                    ea = P.sb([128, 64], F32, "ea")
                    dec = P.sb([128, 64], F32, "dec")
                    te = P.sb([128, 64], F32, "te")
                    Et = [P.sb([128, 128], F32, "Et") for _ in range(2)]
                    WT = [P.sb([128, 128], BF16, "WT") for _ in range(2)]
                    ld = P.sb([128, 4, 128], F32, "ld")
                    pA = P.ps([128, 512], F32, "pA")
                    pT = [P.ps([128, 8, 128], BF16, "pT") for _ in range(2)]
                    pcb = [P.ps([128, 512], F32, "pcb") for _ in range(2)]
                    pa = [P.ps([128, 512], F32, "pa") for _ in range(2)]
                    py = P.ps([128, 512], F32, "py")
                    if d == 1:
                        yf = P.sb([128, 4096], F32, "yf")
                        zb = P.sb([128, 4096], F32, "zb")
                        nrm = load_bc(P, W["ssd_norm"], li * DIN, DIN, "nrm")
                        dsk = load_bc(P, W["d_skip"], li * 64, 64, "dsk")
                        ssy = P.sb([128, 1], F32, "ssy")
                    dma("sp", Sst[:], inits_in[li, d], w=[Sst])
                    op("pool", lambda e: e.tensor_copy(out=Sb[:], in_=Sst[:]), r=[Sst], w=[Sb])
                    order = list(range(NCH)) if d == 0 else list(range(NCH - 1, -1, -1))
                    for idx, c in enumerate(order):
                        if ODD_STOP in (4, 5):
                            break
                        segstart = (c % 2 == 0) if d == 0 else (c % 2 == 1)
                        segend = not segstart
                        sg = c // 2
                        rs = slice(c * 128, (c + 1) * 128)
                        dma("sp", xs[:], xbc_s[rs, 0:4096], w=[xs])
                        dma("act", bcm[:], xbc_s[rs, 4096:6144], w=[bcm])
                        dma("sp", dd[:], dtda_s[rs, :], w=[dd])
                        dt = dd[:, d * 64:(d + 1) * 64]
                        da = dd[:, 128 + d * 64:128 + (d + 1) * 64]
                        op("dve", lambda e: e.tensor_copy(out=dah[:], in_=da), r=[dd], w=[dah])
                        op("dve", lambda e: e.tensor_copy(out=dahf[:], in_=dah[:]), r=[dah], w=[dahf])
                        op("dve", lambda e: e.tensor_tensor(out=dal[:], in0=da, in1=dahf[:], op=ALU.subtract), r=[dd, dahf], w=[dal])
                        op("pe", lambda e: e.matmul(pA[:, 0:64], lhsT=trib[:, d, :], rhs=dah[:], start=True, stop=False), r=[trib, dah], w=[pA], inc=False)
                        op("pe", lambda e: e.matmul(pA[:, 0:64], lhsT=trib[:, d, :], rhs=dal[:], start=False, stop=True), r=[trib, dal], w=[pA], inc=False)
                        op("pe", lambda e: e.matmul(pA[:, 64:128], lhsT=onesb[:], rhs=dah[:], start=True, stop=False), r=[onesb, dah], w=[pA], inc=False)
                        op("pe", lambda e: e.matmul(pA[:, 64:128], lhsT=onesb[:], rhs=dal[:], start=False, stop=True), r=[onesb, dal], w=[pA], inc=False)
                        op("pe", lambda e: e.matmul(pA[0:64, 128:256], lhsT=dah[:], rhs=trib[:, d, :], start=True, stop=False), r=[trib, dah], w=[pA], inc=False)
                        op("pe", lambda e: e.matmul(pA[0:64, 128:256], lhsT=dal[:], rhs=trib[:, d, :], start=False, stop=True), r=[trib, dal], w=[pA])
                        if ODD_STOP == 10:
                            return
                        op("act", lambda e: e.copy(out=acum[:], in_=pA[:, 0:64]), r=[pA], w=[acum])
                        op("dve", lambda e: e.tensor_scalar_mul(out=nacum[:], in0=pA[:, 0:64], scalar1=-1.0), r=[pA], w=[nacum])
                        op("act", lambda e: e.copy(out=aend[:], in_=pA[:, 64:128]), r=[pA], w=[aend])
                        op("dve", lambda e: e.tensor_copy(out=acT[:], in_=pA[0:64, 128:256]), r=[pA], w=[acT])
                        op("dve", lambda e: e.tensor_copy(out=acTh[:], in_=acT[:]), r=[acT], w=[acTh])
                        op("dve", lambda e: e.tensor_copy(out=acThf[:], in_=acTh[:]), r=[acTh], w=[acThf])
                        op("dve", lambda e: e.tensor_tensor(out=acTl[:], in0=acT[:], in1=acThf[:], op=ALU.subtract), r=[acT, acThf], w=[acTl])
                        op("act", lambda e: e.activation(out=ea[:], in_=acum[:], func=AF.Exp), r=[acum], w=[ea])
                        op("act", lambda e: e.activation(out=dec[:], in_=aend[:], func=AF.Exp), r=[aend], w=[dec])
                        op("dve", lambda e: e.tensor_tensor(out=te[:], in0=aend[:], in1=acum[:], op=ALU.subtract), r=[aend, acum], w=[te])
                        op("act", lambda e: e.activation(out=te[:], in_=te[:], func=AF.Exp), r=[te], w=[te])
                        op("dve", lambda e: e.tensor_tensor(out=te[:], in0=te[:], in1=dt, op=ALU.mult), r=[te, dd], w=[te])
                        if ODD_STOP == 11:
                            return
                        if idx > 0 and segstart:
                            op("dve", lambda e: e.tensor_scalar(out=Sst[:], in0=Sst[:], scalar1=flag[:, 0:1], scalar2=None, op0=ALU.mult),
                               r=[Sst, flag], w=[Sst])
                            op("pool", lambda e: e.tensor_copy(out=Sb[:], in_=Sst[:]), r=[Sst], w=[Sb])
                        op("pool", lambda e: e.tensor_copy(out=bcb[:], in_=bcm[:]), r=[bcm], w=[bcb])
                        op("act", lambda e: e.copy(out=xsb[:], in_=xs[:]), r=[xs], w=[xsb])
                        for g in range(8):
                            op("pe", lambda e, g=g: e.transpose(pT[0][:, g, :], bcb[:, g * 128:(g + 1) * 128], identb[:]), r=[bcb, identb], w=[pT[0]],
                               inc=(g == 7))
                        op("act", lambda e: e.copy(out=BT[:], in_=pT[0][:]), r=[pT[0]], w=[BT])
                        for g in range(8):
                            op("pe", lambda e, g=g: e.transpose(pT[1][:, g, :], bcb[:, 1024 + g * 128:1024 + (g + 1) * 128], identb[:]), r=[bcb, identb],
                               w=[pT[1]], inc=(g == 7))
                        op("dve", lambda e: e.tensor_copy(out=CTt[:], in_=pT[1][:]), r=[pT[1]], w=[CTt])
                        if ODD_STOP == 12:
                            return
                        for g in range(8):
                            op("pe", lambda e, g=g: e.matmul(pcb[g // 4][:, (g % 4) * 128:(g % 4 + 1) * 128], lhsT=BT[:, g, :], rhs=CTt[:, g, :],
                                                             start=True, stop=True), r=[BT, CTt], w=[pcb[g // 4]], inc=(g % 4 == 3))
                        op("dve", lambda e: e.tensor_tensor(out=cbT[:, 0:4, :], in0=pcb[0][:, 0:512].rearrange("p (g i) -> p g i", g=4),
                                                            in1=bc_mid(trif[:, d, :], 4), op=ALU.mult), r=[pcb[0], trif], w=[cbT])
                        op("dve", lambda e: e.tensor_tensor(out=cbT[:, 4:8, :], in0=pcb[1][:, 0:512].rearrange("p (g i) -> p g i", g=4),
                                                            in1=bc_mid(trif[:, d, :], 4), op=ALU.mult), r=[pcb[1], trif], w=[cbT])
                        if ODD_STOP == 13:
                            return
                        for g in range(8):
                            pp = pcb[g % 2]
                            op("pe", lambda e: e.matmul(pp[:, 0:512], lhsT=CTt[:, g, :], rhs=Sb[:, g * 512:(g + 1) * 512], start=True, stop=True),
                               r=[CTt, Sb], w=[pp])
                            op("dve", lambda e: e.tensor_tensor(out=yacc[:, g * 512:(g + 1) * 512].rearrange("p (h q) -> p h q", h=8),
                                                                in0=pp[:, 0:512].rearrange("p (h q) -> p h q", h=8),
                                                                in1=bc_last(ea[:, g * 8:(g + 1) * 8], 64), op=ALU.mult), r=[pp, ea], w=[yacc])
                        if ODD_STOP == 14:
                            return
                        def issue_A(h):
                            p_a = pa[h % 2]
                            op("pe", lambda e: e.matmul(p_a[:, 0:128], lhsT=identb[0:64, h:h + 1].broadcast_to([64, 128]), rhs=acTh[:, :],
                                                        start=True, stop=False), r=[identb, acTh], w=[p_a], inc=False)
                            op("pe", lambda e: e.matmul(p_a[:, 0:128], lhsT=identb[0:64, h:h + 1].broadcast_to([64, 128]), rhs=acTl[:, :],
                                                        start=False, stop=True), r=[identb, acTl], w=[p_a])
                        issue_A(0)
                        for h in range(64):
                            g, hh = divmod(h, 8)
                            p_a = pa[h % 2]
                            et = Et[h % 2]
                            wt = WT[h % 2]
                            if h + 1 < 64:
                                issue_A(h + 1)
                            op("act", lambda e: e.activation(out=et[:], in_=p_a[:, 0:128], func=AF.Abs, bias=nacum[:, h:h + 1], scale=1.0),
                               r=[p_a, nacum], w=[et])
                            op("act", lambda e: e.activation(out=et[:], in_=et[:], func=AF.Exp, scale=-1.0), r=[et], w=[et])
                            op("dve", lambda e: e.scalar_tensor_tensor(out=wt[:], in0=et[:], scalar=dd[:, d * 64 + h:d * 64 + h + 1], in1=cbT[:, g, :],
                                                                       op0=ALU.mult, op1=ALU.mult), r=[et, dd, cbT], w=[wt])
                            op("pe", lambda e: e.matmul(py[:, hh * 64:(hh + 1) * 64], lhsT=wt[:], rhs=xsb[:, h * 64:(h + 1) * 64], start=True, stop=True),
                               r=[wt, xsb], w=[py])
                            if hh == 7:
                                op("dve", lambda e: e.tensor_tensor(out=yacc[:, g * 512:(g + 1) * 512], in0=yacc[:, g * 512:(g + 1) * 512], in1=py[:, 0:512],
                                                                    op=ALU.add), r=[yacc, py], w=[yacc])
                        if ODD_STOP == 15:
                            return
                        op("dve", lambda e: e.tensor_tensor(out=xsc[:].rearrange("p (h q) -> p h q", h=64), in0=xs[:].rearrange("p (h q) -> p h q", h=64),
                                                            in1=bc_last(te[:, :], 64), op=ALU.mult), r=[xs, te], w=[xsc])
                        for g in range(8):
                            pp = pcb[g % 2]
                            op("pe", lambda e: e.matmul(pp[:, 0:512], lhsT=bcb[:, g * 128:(g + 1) * 128], rhs=xsc[:, g * 512:(g + 1) * 512], start=True, stop=True),
                               r=[bcb, xsc], w=[pp])
                            sv = Sst[:, g * 512:(g + 1) * 512].rearrange("p (h q) -> p h q", h=8)
                            op("dve", lambda e: e.tensor_tensor(out=sv, in0=sv, in1=bc_last(dec[:, g * 8:(g + 1) * 8], 64), op=ALU.mult), r=[Sst, dec], w=[Sst])
                            op("dve", lambda e: e.tensor_tensor(out=Sst[:, g * 512:(g + 1) * 512], in0=Sst[:, g * 512:(g + 1) * 512], in1=pp[:, 0:512], op=ALU.add),
                               r=[Sst, pp], w=[Sst])
                        op("pool", lambda e: e.tensor_copy(out=Sb[:], in_=Sst[:]), r=[Sst], w=[Sb])
                        if ODD_STOP == 16:
                            return
                        if segend:
                            dma("sp", ssd_out[li, sg, d], Sst[:], r=[Sst])
                        if d == 0:
                            dma("sp", hf_s[rs, 0:4096], yacc[:], r=[yacc])
                        else:
                            dma("act", yf[:], hf_s[rs, 0:4096], w=[yf])
                            dma("act", zb[:], proj_s[2 + c * 128:2 + (c + 1) * 128, 0:4096], w=[zb])
                            op("dve", lambda e: e.tensor_tensor(out=yacc[:], in0=yacc[:], in1=yf[:], op=ALU.add), r=[yacc, yf], w=[yacc])
                            op("pool", lambda e: e.tensor_tensor(out=yf[:].rearrange("p (h q) -> p h q", h=64), in0=xs[:].rearrange("p (h q) -> p h q", h=64),
                                                                 in1=bc_last(dsk[:, :], 64), op=ALU.mult), r=[xs, dsk], w=[yf])
                            op("dve", lambda e: e.tensor_tensor(out=yacc[:], in0=yacc[:], in1=yf[:], op=ALU.add), r=[yacc, yf], w=[yacc])
                            op("act", lambda e: e.activation(out=zb[:], in_=zb[:], func=AF.Silu), r=[zb], w=[zb])
                            op("dve", lambda e: e.tensor_tensor(out=yacc[:], in0=yacc[:], in1=zb[:], op=ALU.mult), r=[yacc, zb], w=[yacc])
                            op("act", lambda e: e.activation(out=zb[:], in_=yacc[:], func=AF.Square, accum_out=ssy[:]), r=[yacc], w=[zb, ssy])
                            rstd_inplace(ssy, DIN)
                            op("dve", lambda e: e.scalar_tensor_tensor(out=yacc[:], in0=yacc[:], scalar=ssy[:, 0:1], in1=nrm[:], op0=ALU.mult, op1=ALU.mult),
                               r=[yacc, ssy, nrm], w=[yacc])
                            dma("sp", cat_s[rs, 0:4096], yacc[:], r=[yacc])
        xsrc = x_in
        for l in layers:
            even = (l % 2 == 0)
            li = l // 2
            Win = W["w_in_even"][li] if even else W["w_in_odd"][li]
            EI = EIN if even else OIN

            for ti in range(NTILE):
                with Phase(kb) as P:
                    hT = P.sb([128, 16, TT], BF16, "hT")
                    tp = [P.ps([128, 8, 128], BF16, "tp") for _ in range(2)]
                    pls = [P.ps([128, 512], F32, "pl") for _ in range(4)]
                    wbufs = [P.sb([128, 16, 512], BF16, "wb") for _ in range(3)]
                    obs = [P.sb([128, 512], F32, "ob") for _ in range(3)]
                    with Phase(kb) as P2:
                        xts = [P2.sb([128, D], F32, "xt") for _ in range(2)]
                        run = make_norm(P2, l, 0, hT, tp)
                        for tc in range(NTC):
                            xt = xts[tc % 2]
                            r0 = ti * TT + tc * 128
                            dma("sp", xt[:], xsrc[r0:r0 + 128, :], w=[xt])
                            run(tc, xt, xt[:])
                    oc = [0]

                    def epi(tc, e0, ec, ps):
                        ob = obs[oc[0] % 3]
                        oc[0] += 1
                        if oc[0] % 2 == 0:
                            op("act", lambda e: e.copy(out=ob[:, 0:ec], in_=ps[:, 0:ec]), r=[ps], w=[ob])
                        else:
                            op("dve", lambda e: e.tensor_copy(out=ob[:, 0:ec], in_=ps[:, 0:ec]), r=[ps], w=[ob])
                        r0 = 2 + ti * TT + tc * 128
                        dma("sp", proj_s[r0:r0 + 128, e0:e0 + ec], ob[:, 0:ec], r=[ob])
                    linear(hT, NTC, 16, Win, EI, wbufs, pls, epi)

            if even:
                even_mixer(l, li)
            else:
                odd_mixer(l, li)
            KC = D if even else DIN
            Wout = W["w_out_even"][li] if even else W["w_out_odd"][li]

            for ti in range(NTILE):
                with Phase(kb) as P:
                    nk = KC // 128
                    ebw = 512 if nk == 16 else 256
                    catT = P.sb([128, nk, TT], BF16, "catT")
                    tp = [P.ps([128, 8, 128], BF16, "tp") for _ in range(2)]
                    pls = [P.ps([128, 512], F32, "pl") for _ in range(4)]
                    wbufs = [P.sb([128, nk, ebw], BF16, "wb") for _ in range(2)]
                    xres = P.sb([128, NTC, D], F32, "xres")
                    g1 = load_bc(P, mod_s, l * 6 * D + 2 * D, D, "g1")
                    tmps = [P.sb([128, 512], F32, "tmp") for _ in range(2)]
                    with Phase(kb) as P2:
                        cf = [P2.sb([128, KC], F32, "cf") for _ in range(2)]
                        cbb = [P2.sb([128, KC], BF16, "cbb") for _ in range(2)]
                        for tc in range(NTC):
                            r0 = ti * TT + tc * 128
                            dma("sp", xres[:, tc, :], xsrc[r0:r0 + 128, :], w=[xres])
                            c_f = cf[tc % 2]
                            c_b = cbb[tc % 2]
                            dma("act", c_f[:], cat_s[r0:r0 + 128, 0:KC], w=[c_f])
                            op("pool", lambda e: e.tensor_copy(out=c_b[:], in_=c_f[:]), r=[c_f], w=[c_b])
                            transpose_to(c_b, KC, catT, tc * 128, tp)
                    oc = [0]

                    def epi(tc, e0, ec, ps):
                        t_ = tmps[oc[0] % 2]
                        oc[0] += 1
                        op("dve", lambda e: e.tensor_tensor(out=t_[:, 0:ec], in0=ps[:, 0:ec], in1=g1[:, e0:e0 + ec], op=ALU.mult),
                           r=[ps, g1], w=[t_])
                        op("dve", lambda e: e.tensor_tensor(out=xres[:, tc, e0:e0 + ec], in0=xres[:, tc, e0:e0 + ec], in1=t_[:, 0:ec],
                                                           op=ALU.add), r=[xres, t_], w=[xres])
                    linear(catT, NTC, nk, Wout, D, wbufs, pls, epi, ebw=ebw)
                    for tc in range(NTC):
                        r0 = ti * TT + tc * 128
                        dma("sp", y_out[r0:r0 + 128, :], xres[:, tc, :], r=[xres])
            xsrc = y_out

            for ti in range(NTILE):
                with Phase(kb) as P:
                    hT = P.sb([128, 16, TT], BF16, "hT")
                    ffT = P.sb([128, 44, TT], BF16, "ffT")
                    tp = [P.ps([128, 8, 128], BF16, "tp") for _ in range(2)]
                    pls = [P.ps([128, 512], F32, "pl") for _ in range(4)]
                    xres = P.sb([128, NTC, D], F32, "xres")
                    with Phase(kb) as P2:
                        run = make_norm(P2, l, 1, hT, tp)
                        for tc in range(NTC):
                            r0 = ti * TT + tc * 128
                            dma("sp", xres[:, tc, :], xsrc[r0:r0 + 128, :], w=[xres])
                            run(tc, xres, xres[:, tc, :])
                    with Phase(kb) as P3:
                        wg = [P3.sb([128, 16, 512], BF16, "wg") for _ in range(2)]
                        wu = [P3.sb([128, 16, 512], BF16, "wu") for _ in range(2)]
                        sgs = [P3.sb([128, 512], F32, "sg") for _ in range(2)]
                        ffb = [P3.sb([128, 512], BF16, "ffb") for _ in range(2)]
                        Wg = W["w_gate"][l].rearrange("(k p) e -> p k e", p=128)
                        Wu = W["w_up"][l].rearrange("(k p) e -> p k e", p=128)
                        cc = 0
                        pend_t = [None]
                        for eb in range(11):
                            e0 = eb * 512
                            wbg = wg[eb % 2]
                            wbu = wu[eb % 2]
                            dma("pool", wbg[:], Wg[:, :, e0:e0 + 512], w=[wbg])
                            dma("pool", wbu[:], Wu[:, :, e0:e0 + 512], w=[wbu])
                            for tc in range(NTC):
                                pg = pls[(cc * 2) % 4]
                                pu = pls[(cc * 2 + 1) % 4]
                                sg = sgs[cc % 2]
                                fb = ffb[cc % 2]
                                pt = tp[cc % 2]
                                cc += 1
                                for k in range(16):
                                    op("pe", lambda e, k=k: e.matmul(pg[:], lhsT=hT[:, k, tc * 128:(tc + 1) * 128], rhs=wbg[:, k, :],
                                                                     start=(k == 0), stop=(k == 15)), r=[hT, wbg], w=[pg], inc=(k == 15))
                                for k in range(16):
                                    op("pe", lambda e, k=k: e.matmul(pu[:], lhsT=hT[:, k, tc * 128:(tc + 1) * 128], rhs=wbu[:, k, :],
                                                                     start=(k == 0), stop=(k == 15)), r=[hT, wbu], w=[pu], inc=(k == 15))
                                if pend_t[0] is not None:
                                    pend_t[0]()
                                    pend_t[0] = None
                                op("act", lambda e: e.activation(out=sg[:], in_=pg[:], func=AF.Silu), r=[pg], w=[sg])
                                op("dve", lambda e: e.tensor_tensor(out=fb[:], in0=sg[:], in1=pu[:], op=ALU.mult), r=[sg, pu], w=[fb])

                                def do_tr(fb=fb, pt=pt, eb=eb, tc=tc):
                                    for k in range(4):
                                        op("pe", lambda e, k=k: e.transpose(pt[:, k, :], fb[:, k * 128:(k + 1) * 128], identb[:]),
                                           r=[fb, identb], w=[pt], inc=(k == 3))
                                    op("act", lambda e: e.copy(out=ffT[:, eb * 4:eb * 4 + 4, tc * 128:(tc + 1) * 128], in_=pt[:, 0:4, :]),
                                       r=[pt], w=[ffT])
                                pend_t[0] = do_tr
                        if pend_t[0] is not None:
                            pend_t[0]()
                            pend_t[0] = None
                    with Phase(kb) as P4:
                        g2 = load_bc(P4, mod_s, l * 6 * D + 5 * D, D, "g2")
                        tmps = [P4.sb([128, 256], F32, "tmp") for _ in range(2)]
                        wd = [P4.sb([128, 44, 256], BF16, "wd") for _ in range(2)]
                        oc = [0]

                        def epi(tc, e0, ec, ps):
                            t_ = tmps[oc[0] % 2]
                            oc[0] += 1
                            op("dve", lambda e: e.tensor_tensor(out=t_[:, 0:ec], in0=ps[:, 0:ec], in1=g2[:, e0:e0 + ec], op=ALU.mult),
                               r=[ps, g2], w=[t_])
                            op("dve", lambda e: e.tensor_tensor(out=xres[:, tc, e0:e0 + ec], in0=xres[:, tc, e0:e0 + ec], in1=t_[:, 0:ec],
                                                               op=ALU.add), r=[xres, t_], w=[xres])
                        linear(ffT, NTC, 44, W["w_down"][l], D, wd, pls, epi, ebw=256)
                        for tc in range(NTC):
                            r0 = ti * TT + tc * 128
                            dma("sp", y_out[r0:r0 + 128, :], xres[:, tc, :], r=[xres])
        G.__exit__(None, None, None)
    return nc, I, O


WSHAPES = [("w_ada", [4, D, 6 * D]), ("b_ada", [4, 6 * D]), ("norm_mix", [4, D]), ("norm_ffn", [4, D]),
           ("w_gate", [4, D, DFF]), ("w_up", [4, D, DFF]), ("w_down", [4, DFF, D]),
           ("w_in_even", [2, D, EIN]), ("b_gate_mlstm", [2, 32]), ("mlstm_norm", [2, 1024]),
           ("q_norm", [2, 64]), ("k_norm", [2, 64]), ("lambda_q1", [2, 64]), ("lambda_k1", [2, 64]),
           ("lambda_q2", [2, 64]), ("lambda_k2", [2, 64]), ("diff_norm", [2, 128]),
           ("w_out_even", [2, D, D]), ("w_in_odd", [2, D, OIN]), ("conv_w", [2, 4, CONVC]),
           ("conv_b", [2, CONVC]), ("dt_bias", [2, 2, 64]), ("a_log", [2, 2, 64]), ("d_skip", [2, 64]),
           ("ssd_norm", [2, DIN]), ("w_out_odd", [2, DIN, D])]


def core_inputs(NSEG, kind, xtok, cvec, weights, ctxk=None, ctxv=None, initC=None, initn=None, initm=None, inits=None):
    NT = NSEG * 256
    NK = 256 + NT
    f = 1.0 if kind == "sample" else 0.0
    bf = ml_dtypes.bfloat16
    m = {}
    m["x"] = np.ascontiguousarray(xtok, dtype=np.float32)
    m["cT"] = np.ascontiguousarray(cvec.reshape(16, 128).T, dtype=np.float32)
    m["flag"] = np.full((128, 1), f, np.float32)
    cm = np.ones((128, 3), np.float32)
    cm[0:2, 0] = f
    cm[0:1, 1] = f
    cm[127, 2] = f
    m["cmask"] = cm
    m["identf"] = np.eye(128, dtype=np.float32)
    m["identb"] = np.eye(128, dtype=np.float32).astype(bf)
    t = np.arange(128)
    tri = np.stack([(t[:, None] <= t[None, :]), (t[:, None] >= t[None, :])]).astype(np.float32)
    m["trif"] = tri
    m["negm"] = ((1.0 - tri) * (-BIG)).astype(bf)
    if kind == "sample":
        pos = np.arange(NT)
        rows = (pos // 64).astype(np.float32)
        cols = (pos % 64).astype(np.float32)
        inv = (10000.0 ** (-np.arange(16, dtype=np.float32) / 16.0)).astype(np.float32)
        ang = np.concatenate([rows[:, None] * inv[None, :], cols[:, None] * inv[None, :]], axis=1).astype(np.float32)
        m["cosT"] = np.cos(ang).astype(np.float32)
        m["sinT"] = np.sin(ang).astype(np.float32)
        m["qseg"] = np.zeros((16, NT), bf)
    else:
        m["cosT"] = np.ones((NT, 32), np.float32)
        m["sinT"] = np.zeros((NT, 32), np.float32)
        q = np.zeros((16, NT), np.float32)
        q[np.arange(NT) // 256, np.arange(NT)] = 1.0
        m["qseg"] = q.astype(bf)
    ks = np.zeros((16, NK), np.float32)
    ks[0:8, 0:256] = -BIG
    sk = np.arange(NT) // 256
    for s in range(NSEG):
        ks[s, 256:] = np.where(sk == s, 0.0, -BIG)
    for s in range(NSEG, 8):
        ks[s, 256:] = 0.0
    m["kseg"] = ks.astype(bf)
    z = lambda *s: np.zeros(s, np.float32)
    m["ctxk"] = z(2, 256, 1024) if ctxk is None else np.ascontiguousarray(ctxk, np.float32)
    m["ctxv"] = z(2, 256, 1024) if ctxv is None else np.ascontiguousarray(ctxv, np.float32)
    m["initC"] = z(2, 2, 8, 128, 64) if initC is None else np.ascontiguousarray(initC, np.float32)
    m["initn"] = z(2, 2, 8, 64) if initn is None else np.ascontiguousarray(initn, np.float32)
    m["initm"] = z(2, 2, 8) if initm is None else np.ascontiguousarray(initm, np.float32)
    m["inits"] = z(2, 2, 128, 4096) if inits is None else np.ascontiguousarray(np.asarray(inits, np.float32).reshape(2, 2, 4096, 128).transpose(0, 1, 3, 2))
    m.update(weights)
    return m


_CACHE = {}


def kernel(**inp):
    NSEG = 8
    inp = {k: np.asarray(v) for k, v in inp.items()}
    weights = {nm: np.ascontiguousarray(inp[nm], dtype=np.float32) for nm, _ in WSHAPES}
    key = (NSEG, (0, 1, 2, 3))
    if key not in _CACHE:
        _CACHE[key] = build(NSEG, [0, 1, 2, 3])
    nc, I, O = _CACHE[key]
    xp = inp["x_prompt"]
    xs = inp["x_sample"]
    counts = [6, 6, 5, 5, 5, 5]
    starts = np.concatenate([[0], np.cumsum(counts)])
    in_maps = []
    for b in range(2):
        in_maps.append(core_inputs(
            NSEG, "sample", xs[b], inp["c"][b], weights,
            ctxk=inp["cache_attn_k"][b].reshape(2, 256, 1024), ctxv=inp["cache_attn_v"][b].reshape(2, 256, 1024),
            initC=inp["state_mlstm_c"][b], initn=inp["state_mlstm_n"][b], initm=inp["state_mlstm_m"][b],
            inits=inp["state_ssd"][b].reshape(2, 2, 4096, 128)))
    for p in range(6):
        xt = np.zeros((NSEG * 256, D), np.float32)
        n = counts[p]
        xt[:n * 256] = xp[starts[p]:starts[p] + n].reshape(n * 256, D)
        in_maps.append(core_inputs(NSEG, "prompt", xt, inp["c_ctx"], weights))
    res = run_bass_kernel_spmd(nc, in_maps, core_ids=list(range(8)))
    R = res.results
    y_sample = np.stack([R[b]["y"] for b in range(2)]).astype(np.float32)
    B = xp.shape[0]
    y_prompt = np.zeros((B, 256, D), np.float32)
    nk = np.zeros((B, 2, 256, 8, 2, 64), np.float32)
    nv = np.zeros((B, 2, 256, 8, 128), np.float32)
    nC = np.zeros((B, 2, 2, 8, 128, 64), np.float32)
    nn = np.zeros((B, 2, 2, 8, 64), np.float32)
    nm_ = np.zeros((B, 2, 2, 8), np.float32)
    ns = np.zeros((B, 2, 2, 64, 64, 128), np.float32)
    for p in range(6):
        r = R[2 + p]
        for s in range(counts[p]):
            q = starts[p] + s
            y_prompt[q] = r["y"][s * 256:(s + 1) * 256]
            nk[q] = r["kd"][:, s * 256:(s + 1) * 256].reshape(2, 256, 8, 2, 64)
            nv[q] = r["vd"][:, s * 256:(s + 1) * 256].reshape(2, 256, 8, 128)
            nC[q] = r["mC"][:, s]
            nn[q] = r["mn"][:, s]
            nm_[q] = r["mm"][:, s]
            ns[q] = r["ssd"][:, s].transpose(0, 1, 3, 2).reshape(2, 2, 64, 64, 128)
    return (y_prompt, y_sample, nk, nv, nC, nn, nm_, ns)
```

```python
import contextlib
import math
import numpy as np
import ml_dtypes
import concourse.bass as bass
import concourse.mybir as mybir
from concourse.bass_utils import run_bass_kernel_spmd

F32 = mybir.dt.float32
BF16 = mybir.dt.bfloat16
AF = mybir.ActivationFunctionType
ALU = mybir.AluOpType
AX = mybir.AxisListType

D = 2048
DFF = 5632
EIN = 6176
OIN = 10368
DIN = 4096
CONVC = 6144
EPS = 1e-6
BIG = 30000.0
ENGS = ["pe", "act", "dve", "pool", "sp"]
NDS = 8
ODD_STOP = 0


class Buf:
    __slots__ = ("t", "w", "r", "excl")

    def __init__(self, t, excl=False):
        self.t = t
        self.w = None
        self.r = {}
        self.excl = excl

    def __getitem__(self, k):
        return self.t[k]


def bc_last(ap, n):
    return bass.AP(ap.tensor, ap.offset, [list(x) for x in ap.ap] + [[0, n]])


def bc_mid(ap, n):
    a = [list(x) for x in ap.ap]
    return bass.AP(ap.tensor, ap.offset, [a[0], [0, n]] + a[1:])


def pbc(t, off, n, parts=128):
    return bass.AP(t, off, [[0, parts], [1, n]])


class KB:
    def __init__(self, nc, es):
        self.nc = nc
        self.e = dict(pe=nc.tensor, act=nc.scalar, dve=nc.vector, pool=nc.gpsimd, sp=nc.sync)
        self.semh = {}
        for k in ENGS:
            self.semh[k] = es.enter_context(nc.semaphore("sem_" + k))
        self.cnt = {k: 0 for k in ENGS}
        self.waited = {}
        self.dqc = {}
        self.dqn = {}
        for q in ("sp", "act", "pool"):
            self.dqc[q] = [0] * NDS
            self.dqn[q] = 0
            for i in range(NDS):
                self.semh[("d", q, i)] = es.enter_context(nc.semaphore("semd_%s_%d" % (q, i)))
        self.pend = {k: ([], []) for k in ENGS}
        self.uid = 0

    def name(self, p):
        self.uid += 1
        return "%s_%d" % (p, self.uid)

    def _deps(self, r, w):
        deps = {}
        for b in r:
            if b.w is not None and deps.get(b.w[0], 0) < b.w[1]:
                deps[b.w[0]] = b.w[1]
            if b.excl:
                for k, v in b.r.items():
                    if deps.get(k, 0) < v:
                        deps[k] = v
        for b in w:
            if b.w is not None and deps.get(b.w[0], 0) < b.w[1]:
                deps[b.w[0]] = b.w[1]
            for k, v in b.r.items():
                if deps.get(k, 0) < v:
                    deps[k] = v
        return deps

    def _wait(self, eng, deps):
        for k, v in deps.items():
            if k == eng and eng == "pe":
                continue
            if self.waited.get((eng, k), 0) >= v:
                continue
            self.e[eng].wait_ge(self.semh[k], v)
            self.waited[(eng, k)] = v

    def op(self, eng, fn, r=(), w=(), inc=True):
        self._wait(eng, self._deps(r, w))
        ins = fn(self.e[eng])
        pr, pw = self.pend[eng]
        if not inc:
            pr.extend(r)
            pw.extend(w)
            return
        self.cnt[eng] += 1
        ins.then_inc(self.semh[eng], 1)
        v = self.cnt[eng]
        for b in list(w) + pw:
            b.w = (eng, v)
            b.r = {}
        for b in list(r) + pr:
            if b.r.get(eng, 0) < v:
                b.r[eng] = v
        self.pend[eng] = ([], [])

    def dma(self, q, out, in_, r=(), w=(), slow=False):
        self._wait(q, self._deps(r, w))
        i = self.dqn[q]
        self.dqn[q] = (i + 1) % NDS
        key = ("d", q, i)
        c = self.dqc[q][i]
        if c > 0 and self.waited.get((q, key), 0) < 16 * c:
            self.e[q].wait_ge(self.semh[key], 16 * c)
            self.waited[(q, key)] = 16 * c
        if slow:
            ins = self.e[q].dma_start(out=out, in_=in_, allow_slow_non_contiguous=True)
        else:
            ins = self.e[q].dma_start(out=out, in_=in_)
        ins.then_inc(self.semh[key], 16)
        self.dqc[q][i] = c + 1
        v = 16 * (c + 1)
        for b in w:
            b.w = (key, v)
            b.r = {}
        for b in r:
            b.r[key] = v

    def barrier(self):
        for k in ENGS:
            assert not self.pend[k][0] and not self.pend[k][1]
        evs = [(k, self.cnt[k]) for k in ENGS if self.cnt[k] > 0]
        for q in self.dqc:
            for i in range(NDS):
                if self.dqc[q][i] > 0:
                    evs.append((("d", q, i), 16 * self.dqc[q][i]))
        for eng in ENGS:
            for k, v in evs:
                if k == eng and eng == "pe":
                    continue
                if self.waited.get((eng, k), 0) >= v:
                    continue
                self.e[eng].wait_ge(self.semh[k], v)
                self.waited[(eng, k)] = v


class Phase:
    def __init__(self, kb):
        self.kb = kb
        self.es = contextlib.ExitStack()

    def __enter__(self):
        self.es.__enter__()
        return self

    def __exit__(self, *a):
        self.kb.barrier()
        return self.es.__exit__(*a)

    def sb(self, shape, dt=F32, name="t"):
        return Buf(self.es.enter_context(self.kb.nc.sbuf_tensor(self.kb.name(name), list(shape), dt)))

    def ps(self, shape, dt=F32, name="p"):
        return Buf(self.es.enter_context(self.kb.nc.psum_tensor(self.kb.name(name), list(shape), dt)), excl=True)


def build(NSEG, layers):
    NT = NSEG * 256
    NCH = NT // 128
    TT = min(512, NT)
    NTILE = NT // TT
    NTC = TT // 128
    NK = 256 + NT
    NKB = NK // 128
    nc = bass.Bass("TRN2", target_bir_lowering=False)
    I = {}
    O = {}

    def din(name, shape, dt=F32):
        I[name] = nc.dram_tensor(name, list(shape), dt, kind="ExternalInput")
        return I[name]

    def dout(name, shape, dt=F32):
        O[name] = nc.dram_tensor(name, list(shape), dt, kind="ExternalOutput")
        return O[name]

    def dscr(name, shape, dt=F32):
        return nc.dram_tensor(name, list(shape), dt, kind="Internal")

    x_in = din("x", [NT, D])
    cT_in = din("cT", [128, 16])
    flag_in = din("flag", [128, 1])
    cmask_in = din("cmask", [128, 3])
    identf_in = din("identf", [128, 128])
    identb_in = din("identb", [128, 128], BF16)
    trif_in = din("trif", [2, 128, 128])
    negm_in = din("negm", [2, 128, 128], BF16)
    cos_in = din("cosT", [NT, 32])
    sin_in = din("sinT", [NT, 32])
    qseg_in = din("qseg", [16, NT], BF16)
    kseg_in = din("kseg", [16, NK], BF16)
    ctxk_in = din("ctxk", [2, 256, 1024])
    ctxv_in = din("ctxv", [2, 256, 1024])
    initC_in = din("initC", [2, 2, 8, 128, 64])
    initn_in = din("initn", [2, 2, 8, 64])
    initm_in = din("initm", [2, 2, 8])
    inits_in = din("inits", [2, 2, 128, 4096])
    W = {}
    for nm, shp in WSHAPES:
        W[nm] = din(nm, shp)

    y_out = dout("y", [NT, D])
    kd_out = dout("kd", [2, NT, 1024])
    vd_out = dout("vd", [2, NT, 1024])
    mC_out = dout("mC", [2, NSEG, 2, 8, 128, 64])
    mn_out = dout("mn", [2, NSEG, 2, 8, 64])
    mm_out = dout("mm", [2, NSEG, 2, 8])
    ssd_out = dout("ssd", [2, NSEG, 2, 128, 4096])

    mod_s = dscr("mod_s", [4, 6 * D])
    proj_s = dscr("proj_s", [NT + 4, OIN])
    cat_s = dscr("cat_s", [NT, DIN])
    hf_s = dscr("hf_s", [NT, DIN])
    xbc_s = dscr("xbc_s", [NT, CONVC])
    dtda_s = dscr("dtda_s", [NT, 256])

    es = contextlib.ExitStack()
    with es:
        kb = KB(nc, es)
        op = kb.op
        dma = kb.dma

        G = Phase(kb)
        G.__enter__()
        identf = G.sb([128, 128], F32, "identf")
        identb = G.sb([128, 128], BF16, "identb")
        trif = G.sb([128, 2, 128], F32, "trif")
        negm = G.sb([128, 2, 128], BF16, "negm")
        onesf = G.sb([128, 128], F32, "onesf")
        flag = G.sb([128, 1], F32, "flag")
        cmask = G.sb([128, 3], F32, "cmask")
        dma("sp", identf[:], identf_in[:], w=[identf])
        dma("sp", identb[:], identb_in[:], w=[identb])
        dma("sp", trif[:], trif_in.ap().rearrange("d t i -> t d i"), w=[trif])
        dma("sp", negm[:], negm_in.ap().rearrange("d t i -> t d i"), w=[negm])
        dma("sp", flag[:], flag_in[:], w=[flag])
        dma("sp", cmask[:], cmask_in[:], w=[cmask])
        op("dve", lambda e: e.memset(onesf[:], 1.0), w=[onesf])
        trib = G.sb([128, 2, 128], BF16, "trib")
        onesb = G.sb([128, 128], BF16, "onesb")
        op("dve", lambda e: e.tensor_copy(out=trib[:], in_=trif[:]), r=[trif], w=[trib])
        op("dve", lambda e: e.memset(onesb[:], 1.0), w=[onesb])

        with Phase(kb) as P:
            cT = P.sb([128, 16], F32)
            sc = P.sb([128, 16], F32)
            dma("sp", cT[:], cT_in[:], w=[cT])
            op("act", lambda e: e.activation(out=sc[:], in_=cT[:], func=AF.Silu), r=[cT], w=[sc])
            wa = [P.sb([128, 16, 512], F32, "wa") for _ in range(2)]
            brow = P.sb([1, 6 * D], F32, "brow")
            mrow = P.sb([1, 6 * D], F32, "mrow")
            pa = [P.ps([128, 512], F32, "pa") for _ in range(2)]
            it = 0
            for l in layers:
                dma("sp", brow[:], W["b_ada"][l:l + 1, :], w=[brow])
                for eb in range(24):
                    wb = wa[it % 2]
                    ps = pa[it % 2]
                    it += 1
                    dma("sp" if eb % 2 == 0 else "act", wb[:],
                        W["w_ada"][l].rearrange("(k p) e -> p k e", p=128)[:, :, eb * 512:(eb + 1) * 512], w=[wb])
                    for k in range(16):
                        op("pe", lambda e, k=k, wb=wb, ps=ps: e.matmul(ps[0:1, :], lhsT=sc[:, k:k + 1], rhs=wb[:, k, :],
                                                                     start=(k == 0), stop=(k == 15)),
                           r=[sc, wb], w=[ps], inc=(k == 15))
                    op("dve", lambda e, ps=ps, eb=eb: e.tensor_tensor(out=mrow[:, eb * 512:(eb + 1) * 512], in0=ps[0:1, :],
                                                                    in1=brow[:, eb * 512:(eb + 1) * 512], op=ALU.add),
                       r=[ps, brow], w=[mrow])
                for j in (1, 4):
                    op("dve", lambda e, j=j: e.tensor_scalar_add(out=mrow[:, j * D:(j + 1) * D], in0=mrow[:, j * D:(j + 1) * D],
                                                                 scalar1=1.0), r=[mrow], w=[mrow])
                dma("sp", mod_s[l:l + 1, :], mrow[:], r=[mrow])
            zt = P.sb([4, OIN], F32, "zt")
            op("dve", lambda e: e.memset(zt[:], 0.0), w=[zt])
            dma("sp", proj_s[0:2, :], zt[0:2, :], r=[zt])
            dma("sp", proj_s[NT + 2:NT + 4, :], zt[2:4, :], r=[zt])

        def load_bc(P, t, off, n, name="bc", q="sp"):
            b = P.sb([128, n], F32, name)
            dma(q, b[:], pbc(t, off, n), w=[b])
            return b

        def rstd_inplace(ss, n):
            op("dve", lambda e: e.tensor_scalar(out=ss[:], in0=ss[:], scalar1=1.0 / n, scalar2=EPS, op0=ALU.mult, op1=ALU.add),
               r=[ss], w=[ss])
            op("act", lambda e: e.activation(out=ss[:], in_=ss[:], func=AF.Sqrt), r=[ss], w=[ss])
            op("dve", lambda e: e.reciprocal(out=ss[:], in_=ss[:]), r=[ss], w=[ss])

        tcnt = [0]

        def transpose_to(src, ncol, dstT, tok0, ptp):
            nkk = ncol // 128
            for k0 in range(0, nkk, 8):
                kn = min(8, nkk - k0)
                pt = ptp[tcnt[0] % len(ptp)]
                tcnt[0] += 1
                for k in range(kn):
                    op("pe", lambda e, k=k, pt=pt: e.transpose(pt[:, k, :], src[:, (k0 + k) * 128:(k0 + k + 1) * 128], identb[:]),
                       r=[src, identb], w=[pt], inc=(k == kn - 1))
                if tcnt[0] % 2 == 0:
                    op("act", lambda e, pt=pt: e.copy(out=dstT[:, k0:k0 + kn, tok0:tok0 + 128], in_=pt[:, 0:kn, :]), r=[pt], w=[dstT])
                else:
                    op("dve", lambda e, pt=pt: e.tensor_copy(out=dstT[:, k0:k0 + kn, tok0:tok0 + 128], in_=pt[:, 0:kn, :]), r=[pt], w=[dstT])

        lcnt = [0]
        wcnt = [0]

        def linear(actT, ntc, nk, Wap, E, wbufs, pls, epi, ebw=512):
            Wv = Wap.rearrange("(k p) e -> p k e", p=128)
            for e0 in range(0, E, ebw):
                ec = min(ebw, E - e0)
                wb = wbufs[wcnt[0] % len(wbufs)]
                wcnt[0] += 1
                dma("pool", wb[:, 0:nk, 0:ec], Wv[:, :, e0:e0 + ec], w=[wb])
                for tc in range(ntc):
                    ps = pls[lcnt[0] % len(pls)]
                    lcnt[0] += 1
                    for k in range(nk):
                        op("pe", lambda e, k=k, ps=ps, wb=wb, tc=tc: e.matmul(ps[:, 0:ec], lhsT=actT[:, k, tc * 128:(tc + 1) * 128],
                                                                             rhs=wb[:, k, 0:ec], start=(k == 0), stop=(k == nk - 1)),
                           r=[actT, wb], w=[ps], inc=(k == nk - 1))
                    epi(tc, e0, ec, ps)

        def make_norm(P, l, which, hT, tp):
            j = 0 if which == 0 else 3
            gm = load_bc(P, mod_s, l * 6 * D + (j + 1) * D, D, "gm")
            sh = load_bc(P, mod_s, l * 6 * D + j * D, D, "sh", q="act")
            gn = load_bc(P, W["norm_mix" if which == 0 else "norm_ffn"], l * D, D, "gn")
            op("dve", lambda e: e.tensor_tensor(out=gm[:], in0=gm[:], in1=gn[:], op=ALU.mult), r=[gm, gn], w=[gm])
            junk = P.sb([128, D], F32, "junk")
            tmp = P.sb([128, D], F32, "tmp")
            hb = [P.sb([128, D], BF16, "hb") for _ in range(2)]
            ss = [P.sb([128, 1], F32, "ss") for _ in range(2)]

            def run(tc, xbuf, xap):
                s_ = ss[tc % 2]
                h_ = hb[tc % 2]
                op("act", lambda e: e.activation(out=junk[:], in_=xap, func=AF.Square, accum_out=s_[:]), r=[xbuf], w=[junk, s_])
                rstd_inplace(s_, D)
                op("dve", lambda e: e.scalar_tensor_tensor(out=tmp[:], in0=xap, scalar=s_[:, 0:1], in1=gm[:], op0=ALU.mult, op1=ALU.mult),
                   r=[xbuf, s_, gm], w=[tmp])
                op("pool", lambda e: e.tensor_tensor(out=h_[:], in0=tmp[:], in1=sh[:], op=ALU.add), r=[tmp, sh], w=[h_])
                transpose_to(h_, D, hT, tc * 128, tp)
            return run
        def even_mixer(l, li):
            lam_init = 0.8 - 0.6 * math.exp(-0.3 * l)
            PM = Phase(kb)
            PM.__enter__()
            gtok = PM.sb([128, NCH, 2, 3, 8], F32, "gtok")
            with Phase(kb) as P:
                bg = load_bc(P, W["b_gate_mlstm"], li * 32, 32, "bg")
                gch = [P.sb([128, 32], F32, "gch") for _ in range(2)]
                e1 = P.sb([128, 16], F32, "e1")
                lfs = [P.sb([128, 16], F32, "lf") for _ in range(2)]
                bT = [P.sb([8, NCH, 128], F32, "bT") for _ in range(2)]
                aT = [P.sb([8, NCH, 128], F32, "aT") for _ in range(2)]
                bend = [P.sb([8, NCH], F32, "bend") for _ in range(2)]
                mx = [P.sb([8, NCH], F32, "mx") for _ in range(2)]
                Mall = [P.sb([8, NCH], F32, "Mall") for _ in range(2)]
                minall = [P.sb([8, NCH], F32, "minall") for _ in range(2)]
                moutall = [P.sb([8, NCH], F32, "moutall") for _ in range(2)]
                negM = [P.sb([8, NCH], F32, "negM") for _ in range(2)]
                biasw = [P.sb([8, NCH], F32, "biasw") for _ in range(2)]
                rr = [P.sb([8, NCH], F32, "rr") for _ in range(2)]
                mseg = [P.sb([8, NSEG], F32, "mseg") for _ in range(2)]
                mi = [P.sb([8, 1], F32, "mi") for _ in range(2)]
                dg = [P.sb([8, 8], F32, "dg") for _ in range(2)]
                pcs = [P.ps([128, 512], F32, "pcs") for _ in range(3)]
                n = 0
                for c in range(NCH):
                    g = gch[c % 2]
                    lf = lfs[c % 2]
                    dma("sp", g[:], proj_s[2 + c * 128:2 + (c + 1) * 128, 3072:3104], w=[g])
                    op("dve", lambda e: e.tensor_tensor(out=g[:], in0=g[:], in1=bg[:], op=ALU.add), r=[g, bg], w=[g])
                    op("act", lambda e: e.activation(out=e1[:], in_=g[:, 16:32], func=AF.Exp, scale=-1.0), r=[g], w=[e1])
                    op("dve", lambda e: e.tensor_scalar_add(out=e1[:], in0=e1[:], scalar1=1.0), r=[e1], w=[e1])
                    op("act", lambda e: e.activation(out=e1[:], in_=e1[:], func=AF.Ln), r=[e1], w=[e1])
                    op("dve", lambda e: e.tensor_scalar_mul(out=lf[:], in0=e1[:], scalar1=-1.0), r=[e1], w=[lf])
                    for d in range(2):
                        ps = pcs[n % 3]
                        n += 1
                        op("pe", lambda e: e.matmul(ps[0:8, 0:128], lhsT=lf[:, d * 8:(d + 1) * 8], rhs=trif[:, d, :], start=True, stop=True),
                           r=[lf, trif], w=[ps], inc=False)
                        op("pe", lambda e: e.matmul(ps[0:8, 128:256], lhsT=g[:, d * 8:(d + 1) * 8], rhs=identf[:], start=True, stop=True),
                           r=[g, identf], w=[ps], inc=False)
                        op("pe", lambda e: e.matmul(ps[0:8, 256:257], lhsT=lf[:, d * 8:(d + 1) * 8], rhs=onesf[:, 0:1], start=True, stop=True),
                           r=[lf, onesf], w=[ps])
                        op("act", lambda e: e.copy(out=bT[d][:, c, :], in_=ps[0:8, 0:128]), r=[ps], w=[bT[d]])
                        op("dve", lambda e: e.tensor_tensor(out=aT[d][:, c, :], in0=ps[0:8, 128:256], in1=bT[d][:, c, :], op=ALU.subtract),
                           r=[ps, bT[d]], w=[aT[d]])
                        op("dve", lambda e: e.tensor_copy(out=bend[d][:, c:c + 1], in_=ps[0:8, 256:257]), r=[ps], w=[bend[d]])
                        op("dve", lambda e: e.tensor_reduce(out=mx[d][:, c:c + 1], in_=aT[d][:, c, :], axis=AX.X, op=ALU.max),
                           r=[aT[d]], w=[mx[d]])
                for d in range(2):
                    order = list(range(NCH)) if d == 0 else list(range(NCH - 1, -1, -1))
                    dma("sp", mi[d][:], bass.AP(initm_in, (li * 2 + d) * 8, [[1, 8], [1, 1]]), w=[mi[d]])
                    prev = None
                    for idx, c in enumerate(order):
                        segstart = (c % 2 == 0) if d == 0 else (c % 2 == 1)
                        segend = not segstart
                        if idx == 0:
                            op("dve", lambda e: e.tensor_copy(out=minall[d][:, c:c + 1], in_=mi[d][:]), r=[mi[d]], w=[minall[d]])
                        elif segstart:
                            op("dve", lambda e: e.tensor_scalar(out=minall[d][:, c:c + 1], in0=moutall[d][:, prev:prev + 1],
                                                                scalar1=flag[0:8, 0:1], scalar2=None, op0=ALU.mult),
                               r=[moutall[d], flag], w=[minall[d]])
                        else:
                            op("dve", lambda e: e.tensor_copy(out=minall[d][:, c:c + 1], in_=moutall[d][:, prev:prev + 1]),
                               r=[moutall[d]], w=[minall[d]])
                        op("dve", lambda e: e.tensor_tensor(out=Mall[d][:, c:c + 1], in0=minall[d][:, c:c + 1], in1=mx[d][:, c:c + 1], op=ALU.max),
                           r=[minall[d], mx[d]], w=[Mall[d]])
                        op("dve", lambda e: e.tensor_tensor(out=moutall[d][:, c:c + 1], in0=Mall[d][:, c:c + 1], in1=bend[d][:, c:c + 1], op=ALU.add),
                           r=[Mall[d], bend[d]], w=[moutall[d]])
                        if segend:
                            sg = c // 2
                            op("dve", lambda e: e.tensor_copy(out=mseg[d][:, sg:sg + 1], in_=moutall[d][:, c:c + 1]), r=[moutall[d]], w=[mseg[d]])
                        prev = c
                    op("dve", lambda e: e.tensor_scalar_mul(out=negM[d][:], in0=Mall[d][:], scalar1=-1.0), r=[Mall[d]], w=[negM[d]])
                    op("dve", lambda e: e.tensor_scalar_add(out=biasw[d][:], in0=negM[d][:], scalar1=math.log(0.125)), r=[negM[d]], w=[biasw[d]])
                    op("dve", lambda e: e.tensor_tensor(out=rr[d][:], in0=minall[d][:], in1=Mall[d][:], op=ALU.subtract),
                       r=[minall[d], Mall[d]], w=[rr[d]])
                    op("act", lambda e: e.activation(out=rr[d][:], in_=rr[d][:], func=AF.Exp), r=[rr[d]], w=[rr[d]])
                    dma("sp", bass.AP(mm_out, li * NSEG * 16 + d * 8, [[1, 8], [16, NSEG]]), mseg[d][:], r=[mseg[d]], slow=True)
                    for c in range(NCH):
                        op("act", lambda e: e.activation(out=aT[d][:, c, :], in_=aT[d][:, c, :], func=AF.Exp, bias=biasw[d][:, c:c + 1], scale=1.0),
                           r=[aT[d], biasw[d]], w=[aT[d]])
                        op("act", lambda e: e.activation(out=bT[d][:, c, :], in_=bT[d][:, c, :], func=AF.Exp, bias=negM[d][:, c:c + 1], scale=-1.0),
                           r=[bT[d], negM[d]], w=[bT[d]])
                        ps = pcs[n % 3]
                        n += 1
                        dgt = dg[c % 2]
                        op("dve", lambda e: e.tensor_scalar(out=dgt[:], in0=identf[0:8, 0:8], scalar1=rr[d][:, c:c + 1], scalar2=None, op0=ALU.mult),
                           r=[identf, rr[d]], w=[dgt])
                        op("pe", lambda e: e.transpose(ps[:, 0:8], aT[d][:, c, :], identf[0:8, 0:8]), r=[aT[d], identf], w=[ps], inc=False)
                        op("pe", lambda e: e.transpose(ps[:, 8:16], bT[d][:, c, :], identf[0:8, 0:8]), r=[bT[d], identf], w=[ps], inc=False)
                        op("pe", lambda e: e.matmul(ps[:, 16:24], lhsT=onesf[0:8, :], rhs=dgt[:], start=True, stop=True), r=[onesf, dgt], w=[ps])
                        op("dve", lambda e: e.tensor_copy(out=gtok[:, c, d, :, :], in_=ps[:, 0:24].rearrange("p (a h) -> p a h", a=3)),
                           r=[ps], w=[gtok])

            for d in range(2):
                with Phase(kb) as P:
                    CT = P.sb([64, 8, 129], F32, "CT")
                    CR = P.sb([64, 8, 129], F32, "CR")
                    CRb = P.sb([64, 8, 129], BF16, "CRb")
                    qkv = [P.sb([128, 2048], F32, "qkv") for _ in range(2)]
                    qb = P.sb([128, 512], BF16, "qb")
                    kbb = P.sb([128, 512], BF16, "kbb")
                    QT = P.sb([64, 8, 128], BF16, "QT")
                    KT = P.sb([64, 8, 128], BF16, "KT")
                    KW = P.sb([128, 8, 64], BF16, "KW")
                    VA = [P.sb([128, 8, 129], BF16, "VA") for _ in range(2)]
                    ST = [P.sb([128, 128], BF16, "ST") for _ in range(2)]
                    dn = [P.sb([128, 1], F32, "dn") for _ in range(2)]
                    hd = [P.sb([128, 1024], F32, "hd") for _ in range(2)]
                    ci = P.sb([128, 8, 64], F32, "ci")
                    co = P.sb([128, 8, 64], F32, "co")
                    psq = [P.ps([128, 8, 128], BF16, "psq") for _ in range(2)]
                    pss = [P.ps([128, 512], F32, "pss") for _ in range(2)]
                    psn = [P.ps([128, 512], F32, "psn") for _ in range(2)]
                    psu = P.ps([128, 512], F32, "psu")
                    pco = P.ps([128, 8, 64], F32, "pco")
                    if d == 1:
                        hfb = P.sb([128, 1024], F32, "hfb")
                        aob = P.sb([128, 1024], F32, "aob")
                        sqb = P.sb([128, 1024], F32, "sqb")
                        ssq = P.sb([128, 8], F32, "ssq")
                        mnb = load_bc(P, W["mlstm_norm"], li * 1024, 1024, "mnb")
                    for b in VA:
                        op("dve", lambda e, b=b: e.memset(b[:, :, 128:129], 1.0), w=[b])
                    dma("sp", ci[:], initC_in[li, d].rearrange("h v k -> v h k"), w=[ci])
                    for hh in range(0, 8, 4):
                        for j in range(4):
                            op("pe", lambda e, j=j: e.transpose(psn[0][0:64, j * 128:(j + 1) * 128], ci[:, hh + j, :], identf[:]),
                               r=[ci, identf], w=[psn[0]], inc=(j == 3))
                        op("dve", lambda e: e.tensor_copy(out=CT[:, hh:hh + 4, 0:128], in_=psn[0][0:64, 0:512].rearrange("p (h v) -> p h v", h=4)),
                           r=[psn[0]], w=[CT])
                    dma("sp", CT[:, :, 128:129], bass.AP(initn_in, (li * 2 + d) * 512, [[1, 64], [64, 8], [1, 1]]), w=[CT], slow=True)
                    order = list(range(NCH)) if d == 0 else list(range(NCH - 1, -1, -1))
                    for idx, c in enumerate(order):
                        segstart = (c % 2 == 0) if d == 0 else (c % 2 == 1)
                        segend = not segstart
                        sg = c // 2
                        buf = qkv[idx % 2]
                        va = VA[idx % 2]
                        hdb = hd[idx % 2]
                        r0 = 2 + c * 128
                        dma("sp", buf[:], proj_s[r0:r0 + 128, 0:2048], w=[buf])
                        if d == 1:
                            dma("act", hfb[:], hf_s[c * 128:(c + 1) * 128, 0:1024], w=[hfb])
                            dma("act", aob[:], proj_s[r0:r0 + 128, 2048:3072], w=[aob])
                        op("act", lambda e: e.copy(out=qb[:], in_=buf[:, 0:512]), r=[buf], w=[qb])
                        op("pool", lambda e: e.tensor_copy(out=kbb[:], in_=buf[:, 512:1024]), r=[buf], w=[kbb])
                        for h in range(8):
                            op("pe", lambda e, h=h: e.transpose(psq[0][0:64, h, :], qb[:, h * 64:(h + 1) * 64], identb[:]),
                               r=[qb, identb], w=[psq[0]], inc=(h == 7))
                        op("act", lambda e: e.copy(out=QT[:], in_=psq[0][0:64, :, :]), r=[psq[0]], w=[QT])
                        for h in range(8):
                            op("pe", lambda e, h=h: e.transpose(psq[1][0:64, h, :], kbb[:, h * 64:(h + 1) * 64], identb[:]),
                               r=[kbb, identb], w=[psq[1]], inc=(h == 7))
                        op("dve", lambda e: e.tensor_copy(out=KT[:], in_=psq[1][0:64, :, :]), r=[psq[1]], w=[KT])
                        op("dve", lambda e: e.tensor_tensor(out=KW[:], in0=buf[:, 512:1024].rearrange("p (h k) -> p h k", h=8),
                                                            in1=bc_last(gtok[:, c, d, 0, :], 64), op=ALU.mult), r=[buf, gtok], w=[KW])
                        op("act", lambda e: e.copy(out=va[:, :, 0:128], in_=buf[:, 1024:2048].rearrange("p (h v) -> p h v", h=8)), r=[buf], w=[va])
                        if idx > 0 and segstart:
                            op("dve", lambda e: e.tensor_scalar(out=CT[:], in0=CT[:], scalar1=flag[0:64, 0:1], scalar2=None, op0=ALU.mult),
                               r=[CT, flag], w=[CT])
                        op("dve", lambda e: e.tensor_tensor(out=CR[:], in0=CT[:], in1=bc_last(gtok[0:64, c, d, 2, :], 129), op=ALU.mult),
                           r=[CT, gtok], w=[CR])
                        op("pool", lambda e: e.tensor_copy(out=CRb[:], in_=CR[:]), r=[CR], w=[CRb])
                        def issue_S(h):
                            p_s = pss[h % 2]
                            op("pe", lambda e: e.matmul(p_s[:, 0:128], lhsT=KT[:, h, :], rhs=QT[:, h, :], start=True, stop=True),
                               r=[KT, QT], w=[p_s])
                        issue_S(0)
                        for h in range(8):
                            p_s = pss[h % 2]
                            p_n = psn[h % 2]
                            st = ST[h % 2]
                            dnn = dn[h % 2]
                            if h + 1 < 8:
                                issue_S(h + 1)
                            op("dve", lambda e: e.scalar_tensor_tensor(out=st[:], in0=p_s[:, 0:128], scalar=gtok[:, c, d, 0, h:h + 1],
                                                                       in1=trif[:, d, :], op0=ALU.mult, op1=ALU.mult),
                               r=[p_s, gtok, trif], w=[st])
                            op("pe", lambda e: e.matmul(p_n[:, 0:129], lhsT=st[:], rhs=va[:, h, :], start=True, stop=False),
                               r=[st, va], w=[p_n], inc=False)
                            op("pe", lambda e: e.matmul(p_n[:, 0:129], lhsT=QT[:, h, :], rhs=CRb[:, h, :], start=False, stop=True),
                               r=[QT, CRb], w=[p_n])
                            op("act", lambda e: e.activation(out=dnn[:], in_=p_n[:, 128:129], func=AF.Abs), r=[p_n], w=[dnn])
                            op("dve", lambda e: e.tensor_tensor(out=dnn[:], in0=dnn[:], in1=gtok[:, c, d, 1, h:h + 1], op=ALU.max), r=[dnn, gtok], w=[dnn])
                            op("dve", lambda e: e.reciprocal(out=dnn[:], in_=dnn[:]), r=[dnn], w=[dnn])
                            op("act", lambda e: e.activation(out=hdb[:, h * 128:(h + 1) * 128], in_=p_n[:, 0:128], func=AF.Copy, scale=dnn[:, 0:1]),
                               r=[p_n, dnn], w=[hdb])
                            op("pe", lambda e: e.matmul(psu[0:64, 0:129], lhsT=KW[:, h, :], rhs=va[:, h, :], start=True, stop=True),
                               r=[KW, va], w=[psu])
                            op("dve", lambda e: e.tensor_tensor(out=CT[:, h, :], in0=CR[:, h, :], in1=psu[0:64, 0:129], op=ALU.add),
                               r=[CR, psu], w=[CT])
                        if segend:
                            for h in range(8):
                                op("pe", lambda e, h=h: e.transpose(pco[:, h, :], CT[:, h, 0:128], identf[0:64, 0:64]),
                                   r=[CT, identf], w=[pco], inc=(h == 7))
                            op("act", lambda e: e.copy(out=co[:], in_=pco[:]), r=[pco], w=[co])
                            dma("sp", mC_out[li, sg, d].rearrange("h v k -> v h k"), co[:], r=[co])
                            dma("sp", bass.AP(mn_out, ((li * NSEG + sg) * 2 + d) * 512, [[1, 64], [64, 8], [1, 1]]), CT[:, :, 128:129],
                                r=[CT], slow=True)
                        if d == 0:
                            dma("sp", hf_s[c * 128:(c + 1) * 128, 0:1024], hdb[:], r=[hdb])
                        else:
                            op("dve", lambda e: e.tensor_tensor(out=hdb[:], in0=hdb[:], in1=hfb[:], op=ALU.add), r=[hdb, hfb], w=[hdb])
                            op("pool", lambda e: e.tensor_tensor(out=sqb[:], in0=hdb[:], in1=hdb[:], op=ALU.mult), r=[hdb], w=[sqb])
                            op("dve", lambda e: e.tensor_reduce(out=ssq[:], in_=sqb[:].rearrange("p (h v) -> p h v", h=8), axis=AX.X, op=ALU.add),
                               r=[sqb], w=[ssq])
                            rstd_inplace(ssq, 128)
                            op("dve", lambda e: e.tensor_tensor(out=hdb[:].rearrange("p (h v) -> p h v", h=8),
                                                                in0=hdb[:].rearrange("p (h v) -> p h v", h=8), in1=bc_last(ssq[:, :], 128), op=ALU.mult),
                               r=[hdb, ssq], w=[hdb])
                            op("pool", lambda e: e.tensor_tensor(out=hdb[:], in0=hdb[:], in1=mnb[:], op=ALU.mult), r=[hdb, mnb], w=[hdb])
                            op("act", lambda e: e.activation(out=aob[:], in_=aob[:], func=AF.Sigmoid), r=[aob], w=[aob])
                            op("dve", lambda e: e.tensor_tensor(out=hdb[:], in0=hdb[:], in1=aob[:], op=ALU.mult), r=[hdb, aob], w=[hdb])
                            dma("sp", cat_s[c * 128:(c + 1) * 128, 0:1024], hdb[:], r=[hdb])
            PM.__exit__(None, None, None)

            with Phase(kb) as P:
                QTa = P.sb([128, 8, NT], BF16, "QTa")
                KTa = P.sb([128, 8, NK], BF16, "KTa")
                VAa = P.sb([128, NKB, 8, 129], BF16, "VAa")
                qseg = P.sb([16, NT], BF16, "qseg")
                kseg = P.sb([16, NK], BF16, "kseg")
                dma("sp", qseg[:], qseg_in[:], w=[qseg])
                dma("sp", kseg[:], kseg_in[:], w=[kseg])
                lq = [load_bc(P, W[nm], li * 64, 64, nm) for nm in ("lambda_q1", "lambda_k1", "lambda_q2", "lambda_k2")]
                lsum = [P.sb([128, 1], F32, "lsum") for _ in range(2)]
                neglam = P.sb([128, 1], F32, "neglam")
                for j in range(2):
                    op("dve", lambda e, j=j: e.tensor_tensor(out=lq[2 * j][:], in0=lq[2 * j][:], in1=lq[2 * j + 1][:], op=ALU.mult),
                       r=[lq[2 * j], lq[2 * j + 1]], w=[lq[2 * j]])
                    op("dve", lambda e, j=j: e.tensor_reduce(out=lsum[j][:], in_=lq[2 * j][:], axis=AX.X, op=ALU.add), r=[lq[2 * j]], w=[lsum[j]])
                    op("act", lambda e, j=j: e.activation(out=lsum[j][:], in_=lsum[j][:], func=AF.Exp), r=[lsum[j]], w=[lsum[j]])
                op("dve", lambda e: e.tensor_tensor(out=neglam[:], in0=lsum[1][:], in1=lsum[0][:], op=ALU.subtract), r=lsum, w=[neglam])
                op("dve", lambda e: e.tensor_scalar_add(out=neglam[:], in0=neglam[:], scalar1=-lam_init), r=[neglam], w=[neglam])
                qnb = load_bc(P, W["q_norm"], li * 64, 64, "qnb")
                knb = load_bc(P, W["k_norm"], li * 64, 64, "knb")
                dnb = load_bc(P, W["diff_norm"], li * 128, 128, "dnb")
                op("dve", lambda e: e.tensor_scalar_mul(out=dnb[:], in0=dnb[:], scalar1=(1.0 - lam_init)), r=[dnb], w=[dnb])
                op("dve", lambda e: e.memset(VAa[:, :, :, 128:129], 1.0), w=[VAa])
                with Phase(kb) as P1:
                    raws = [P1.sb([128, 3072], F32, "raw") for _ in range(2)]
                    sqq = P1.sb([128, 2048], F32, "sqq")
                    ssq = P1.sb([128, 32], F32, "ssq")
                    qk = P1.sb([128, 2048], F32, "qk")
                    cst = [P1.sb([128, 32], F32, "cst") for _ in range(2)]
                    snt = [P1.sb([128, 32], F32, "snt") for _ in range(2)]
                    rot = P1.sb([128, 2048], BF16, "rot")
                    t1 = P1.sb([128, 1024], F32, "t1")
                    t2 = P1.sb([128, 1024], F32, "t2")
                    ptr = [P1.ps([128, 8, 128], BF16, "ptr") for _ in range(2)]
                    for c in range(NCH):
                        raw = raws[c % 2]
                        cs_ = cst[c % 2]
                        sn_ = snt[c % 2]
                        r0 = 2 + c * 128
                        dma("sp", raw[:], proj_s[r0:r0 + 128, 3104:6176], w=[raw])
                        dma("act", cs_[:], cos_in[c * 128:(c + 1) * 128, :], w=[cs_])
                        dma("act", sn_[:], sin_in[c * 128:(c + 1) * 128, :], w=[sn_])
                        op("pool", lambda e: e.tensor_tensor(out=sqq[:], in0=raw[:, 0:2048], in1=raw[:, 0:2048], op=ALU.mult), r=[raw], w=[sqq])
                        op("dve", lambda e: e.tensor_reduce(out=ssq[:], in_=sqq[:].rearrange("p (g k) -> p g k", g=32), axis=AX.X, op=ALU.add),
                           r=[sqq], w=[ssq])
                        rstd_inplace(ssq, 64)
                        op("dve", lambda e: e.tensor_tensor(out=qk[:].rearrange("p (g k) -> p g k", g=32),
                                                            in0=raw[:, 0:2048].rearrange("p (g k) -> p g k", g=32),
                                                            in1=bc_last(ssq[:, :], 64), op=ALU.mult), r=[raw, ssq], w=[qk])
                        op("dve", lambda e: e.tensor_tensor(out=qk[:, 0:1024].rearrange("p (g k) -> p g k", g=16),
                                                            in0=qk[:, 0:1024].rearrange("p (g k) -> p g k", g=16),
                                                            in1=bc_mid(qnb[:, :], 16), op=ALU.mult), r=[qk, qnb], w=[qk])
                        op("pool", lambda e: e.tensor_tensor(out=qk[:, 1024:2048].rearrange("p (g k) -> p g k", g=16),
                                                             in0=qk[:, 1024:2048].rearrange("p (g k) -> p g k", g=16),
                                                             in1=bc_mid(knb[:, :], 16), op=ALU.mult), r=[qk, knb], w=[qk])
                        dma("sp", kd_out[li, c * 128:(c + 1) * 128, :], qk[:, 1024:2048], r=[qk])
                        dma("sp", vd_out[li, c * 128:(c + 1) * 128, :], raw[:, 2048:3072], r=[raw])
                        x5 = qk[:, :].rearrange("p (g a t r) -> p g a t r", g=32, a=2, t=2, r=16)
                        r5 = rot[:, :].rearrange("p (g a t r) -> p g a t r", g=32, a=2, t=2, r=16)
                        u1, u2 = x5[:, :, :, 0, :], x5[:, :, :, 1, :]
                        o1, o2 = r5[:, :, :, 0, :], r5[:, :, :, 1, :]
                        cb_ = bc_mid(cs_[:, :].rearrange("p (a r) -> p a r", a=2), 32)
                        sb_ = bc_mid(sn_[:, :].rearrange("p (a r) -> p a r", a=2), 32)
                        t1v = t1[:, :].rearrange("p (g a r) -> p g a r", g=32, a=2)
                        t2v = t2[:, :].rearrange("p (g a r) -> p g a r", g=32, a=2)
                        op("dve", lambda e: e.tensor_tensor(out=t1v, in0=u1, in1=cb_, op=ALU.mult), r=[qk, cs_], w=[t1])
                        op("pool", lambda e: e.tensor_tensor(out=t2v, in0=u2, in1=sb_, op=ALU.mult), r=[qk, sn_], w=[t2])
                        op("dve", lambda e: e.tensor_tensor(out=o1, in0=t1v, in1=t2v, op=ALU.subtract), r=[t1, t2], w=[rot])
                        op("dve", lambda e: e.tensor_tensor(out=t1v, in0=u2, in1=cb_, op=ALU.mult), r=[qk, cs_], w=[t1])
                        op("pool", lambda e: e.tensor_tensor(out=t2v, in0=u1, in1=sb_, op=ALU.mult), r=[qk, sn_], w=[t2])
                        op("dve", lambda e: e.tensor_tensor(out=o2, in0=t1v, in1=t2v, op=ALU.add), r=[t1, t2], w=[rot])
                        for h in range(8):
                            op("pe", lambda e, h=h: e.transpose(ptr[0][:, h, :], rot[:, h * 128:(h + 1) * 128], identb[:]),
                               r=[rot, identb], w=[ptr[0]], inc=(h == 7))
                        op("act", lambda e: e.copy(out=QTa[:, :, c * 128:(c + 1) * 128], in_=ptr[0][:]), r=[ptr[0]], w=[QTa])
                        for h in range(8):
                            op("pe", lambda e, h=h: e.transpose(ptr[1][:, h, :], rot[:, 1024 + h * 128:1024 + (h + 1) * 128], identb[:]),
                               r=[rot, identb], w=[ptr[1]], inc=(h == 7))
                        op("dve", lambda e: e.tensor_copy(out=KTa[:, :, 256 + c * 128:256 + (c + 1) * 128], in_=ptr[1][:]), r=[ptr[1]], w=[KTa])
                        op("act", lambda e: e.copy(out=VAa[:, 2 + c, :, 0:128], in_=raw[:, 2048:3072].rearrange("p (h v) -> p h v", h=8)),
                           r=[raw], w=[VAa])
                    for cb2 in range(2):
                        raw = raws[cb2]
                        dma("sp", raw[:, 0:1024], ctxk_in[li, cb2 * 128:(cb2 + 1) * 128, :], w=[raw])
                        dma("act", raw[:, 2048:3072], ctxv_in[li, cb2 * 128:(cb2 + 1) * 128, :], w=[raw])
                        op("pool", lambda e: e.tensor_copy(out=rot[:, 0:1024], in_=raw[:, 0:1024]), r=[raw], w=[rot])
                        for h in range(8):
                            op("pe", lambda e, h=h: e.transpose(ptr[1][:, h, :], rot[:, h * 128:(h + 1) * 128], identb[:]),
                               r=[rot, identb], w=[ptr[1]], inc=(h == 7))
                        op("dve", lambda e: e.tensor_copy(out=KTa[:, :, cb2 * 128:(cb2 + 1) * 128], in_=ptr[1][:]), r=[ptr[1]], w=[KTa])
                        op("act", lambda e: e.copy(out=VAa[:, cb2, :, 0:128], in_=raw[:, 2048:3072].rearrange("p (h v) -> p h v", h=8)),
                           r=[raw], w=[VAa])
                with Phase(kb) as P2:
                    QB = TT
                    nqc = QB // 128
                    pS = [P2.ps([128, 512], F32, "pS") for _ in range(2)]
                    acc = [P2.ps([128, 512], F32, "acc") for _ in range(4)]
                    Eb = [P2.sb([128, 512], BF16, "Eb") for _ in range(2)]
                    o0 = P2.sb([128, 4, 128], F32, "o0")
                    od = P2.sb([128, 4, 1024], F32, "od")
                    rec = [P2.sb([128, 1], F32, "rec") for _ in range(4)]
                    tt = [P2.sb([128, 128], F32, "tt") for _ in range(2)]
                    sq2 = P2.sb([128, 1024], F32, "sq2")
                    ss2 = P2.sb([128, 8], F32, "ss2")
                    def issue_S(i, qbi, h, c2, kbi):
                        ps = pS[i % 2]
                        op("pe", lambda e: e.matmul(ps[:, 0:QB], lhsT=KTa[c2 * 64:(c2 + 1) * 64, h, kbi * 128:(kbi + 1) * 128],
                                                    rhs=QTa[c2 * 64:(c2 + 1) * 64, h, qbi * QB:(qbi + 1) * QB], start=True, stop=False),
                           r=[KTa, QTa], w=[ps], inc=False)
                        op("pe", lambda e: e.matmul(ps[:, 0:QB], lhsT=kseg[:, kbi * 128:(kbi + 1) * 128],
                                                    rhs=qseg[:, qbi * QB:(qbi + 1) * QB], start=False, stop=True),
                           r=[kseg, qseg], w=[ps])
                    for qbi in range(NT // QB):
                        seq = [(h, c2, kbi) for h in range(8) for c2 in range(2) for kbi in range(NKB)]
                        issue_S(0, qbi, *seq[0])
                        for i, (h, c2, kbi) in enumerate(seq):
                            ps = pS[i % 2]
                            eb = Eb[i % 2]
                            if i + 1 < len(seq):
                                issue_S(i + 1, qbi, *seq[i + 1])
                            op("act", lambda e: e.activation(out=eb[:, 0:QB], in_=ps[:, 0:QB], func=AF.Exp, scale=0.125), r=[ps], w=[eb])
                            for qc in range(nqc):
                                op("pe", lambda e, qc=qc: e.matmul(acc[qc][:, 0:129], lhsT=eb[:, qc * 128:(qc + 1) * 128], rhs=VAa[:, kbi, h, :],
                                                                   start=(kbi == 0), stop=(kbi == NKB - 1)),
                                   r=[eb, VAa], w=[acc[qc]], inc=(qc == nqc - 1))
                            if kbi == NKB - 1:
                                for qc in range(nqc):
                                    rc = rec[qc]
                                    op("dve", lambda e: e.reciprocal(out=rc[:], in_=acc[qc][:, 128:129]), r=[acc[qc]], w=[rc])
                                    if c2 == 0:
                                        op("act", lambda e: e.activation(out=o0[:, qc, :], in_=acc[qc][:, 0:128], func=AF.Copy, scale=rc[:, 0:1]),
                                           r=[acc[qc], rc], w=[o0])
                                    else:
                                        t_ = tt[qc % 2]
                                        op("act", lambda e: e.activation(out=t_[:], in_=acc[qc][:, 0:128], func=AF.Copy, scale=rc[:, 0:1]),
                                           r=[acc[qc], rc], w=[t_])
                                        op("dve", lambda e: e.scalar_tensor_tensor(out=od[:, qc, h * 128:(h + 1) * 128], in0=t_[:], scalar=neglam[:, 0:1],
                                                                                   in1=o0[:, qc, :], op0=ALU.mult, op1=ALU.add),
                                           r=[t_, neglam, o0], w=[od])
                        for qc in range(nqc):
                            op("pool", lambda e: e.tensor_tensor(out=sq2[:], in0=od[:, qc, :], in1=od[:, qc, :], op=ALU.mult), r=[od], w=[sq2])
                            op("dve", lambda e: e.tensor_reduce(out=ss2[:], in_=sq2[:].rearrange("p (h v) -> p h v", h=8), axis=AX.X, op=ALU.add),
                               r=[sq2], w=[ss2])
                            rstd_inplace(ss2, 128)
                            op("dve", lambda e: e.tensor_tensor(out=sq2[:].rearrange("p (h v) -> p h v", h=8),
                                                                in0=od[:, qc, :].rearrange("p (h v) -> p h v", h=8), in1=bc_last(ss2[:, :], 128), op=ALU.mult),
                               r=[od, ss2], w=[sq2])
                            op("pool", lambda e: e.tensor_tensor(out=sq2[:].rearrange("p (h v) -> p h v", h=8),
                                                                 in0=sq2[:].rearrange("p (h v) -> p h v", h=8), in1=bc_mid(dnb[:, :], 8), op=ALU.mult),
                               r=[sq2, dnb], w=[sq2])
                            t0 = qbi * QB + qc * 128
                            dma("sp", cat_s[t0:t0 + 128, 1024:2048], sq2[:], r=[sq2])
        def odd_mixer(l, li):
            for cbk in range(3):
                with Phase(kb) as P:
                    c0 = cbk * 2048
                    wk = P.sb([128, 4, 2048], F32, "wk")
                    for k in range(4):
                        dma("sp", wk[:, k, :], pbc(W["conv_w"], (li * 4 + k) * CONVC + c0, 2048), w=[wk])
                    bb = load_bc(P, W["conv_b"], li * CONVC + c0, 2048, "bb")
                    xk = [[P.sb([128, 2048], F32, "xk") for _ in range(4)] for _ in range(2)]
                    accs = [P.sb([128, 2048], F32, "acc") for _ in range(2)]
                    prs = [P.sb([128, 2048], F32, "pr") for _ in range(2)]
                    for c in range(NCH):
                        xs_ = xk[c % 2]
                        acc = accs[c % 2]
                        segstart = (c % 2 == 0)
                        for k in range(4):
                            dma("sp" if k % 2 == 0 else "act", xs_[k][:], proj_s[c * 128 + k:c * 128 + k + 128, 4096 + c0:4096 + c0 + 2048], w=[xs_[k]])

                        def tap(eng, out, k, mcol):
                            if mcol is None:
                                op(eng, lambda e: e.tensor_tensor(out=out[:], in0=xs_[k][:], in1=wk[:, k, :], op=ALU.mult), r=[xs_[k], wk], w=[out])
                            else:
                                op("dve", lambda e: e.scalar_tensor_tensor(out=out[:], in0=xs_[k][:], scalar=cmask[:, mcol:mcol + 1], in1=wk[:, k, :],
                                                                         op0=ALU.mult, op1=ALU.mult), r=[xs_[k], wk, cmask], w=[out])
                        tap("dve", acc, 0, 0 if segstart else None)
                        tap("pool", prs[0], 1, 1 if segstart else None)
                        op("dve", lambda e: e.tensor_tensor(out=acc[:], in0=acc[:], in1=prs[0][:], op=ALU.add), r=[acc, prs[0]], w=[acc])
                        tap("pool", prs[1], 2, None)
                        op("dve", lambda e: e.tensor_tensor(out=acc[:], in0=acc[:], in1=prs[1][:], op=ALU.add), r=[acc, prs[1]], w=[acc])
                        tap("pool", prs[0], 3, None if segstart else 2)
                        op("dve", lambda e: e.tensor_tensor(out=acc[:], in0=acc[:], in1=prs[0][:], op=ALU.add), r=[acc, prs[0]], w=[acc])
                        op("dve", lambda e: e.tensor_tensor(out=acc[:], in0=acc[:], in1=bb[:], op=ALU.add), r=[acc, bb], w=[acc])
                        op("act", lambda e: e.activation(out=acc[:], in_=acc[:], func=AF.Silu), r=[acc], w=[acc])
                        dma("sp", xbc_s[c * 128:(c + 1) * 128, c0:c0 + 2048], acc[:], r=[acc])
            if ODD_STOP == 1:
                return
            with Phase(kb) as P:
                dtb = load_bc(P, W["dt_bias"], li * 128, 128, "dtb")
                Ab = load_bc(P, W["a_log"], li * 128, 128, "Ab")
                op("act", lambda e: e.activation(out=Ab[:], in_=Ab[:], func=AF.Exp), r=[Ab], w=[Ab])
                op("dve", lambda e: e.tensor_scalar_mul(out=Ab[:], in0=Ab[:], scalar1=-1.0), r=[Ab], w=[Ab])
                dtr = [P.sb([128, 128], F32, "dtr") for _ in range(2)]
                ax = [P.sb([128, 128], F32, "ax") for _ in range(2)]
                dd = [P.sb([128, 256], F32, "dd") for _ in range(2)]
                for c in range(NCH):
                    t_, a_, d_ = dtr[c % 2], ax[c % 2], dd[c % 2]
                    dma("sp", t_[:], proj_s[2 + c * 128:2 + (c + 1) * 128, 10240:10368], w=[t_])
                    op("dve", lambda e: e.tensor_tensor(out=t_[:], in0=t_[:], in1=dtb[:], op=ALU.add), r=[t_, dtb], w=[t_])
                    op("act", lambda e: e.activation(out=a_[:], in_=t_[:], func=AF.Abs), r=[t_], w=[a_])
                    op("act", lambda e: e.activation(out=a_[:], in_=a_[:], func=AF.Exp, scale=-1.0), r=[a_], w=[a_])
                    op("dve", lambda e: e.tensor_scalar_add(out=a_[:], in0=a_[:], scalar1=1.0), r=[a_], w=[a_])
                    op("act", lambda e: e.activation(out=a_[:], in_=a_[:], func=AF.Ln), r=[a_], w=[a_])
                    op("dve", lambda e: e.tensor_scalar_max(out=t_[:], in0=t_[:], scalar1=0.0), r=[t_], w=[t_])
                    op("dve", lambda e: e.tensor_tensor(out=d_[:, 0:128], in0=t_[:], in1=a_[:], op=ALU.add), r=[t_, a_], w=[d_])
                    op("dve", lambda e: e.tensor_tensor(out=d_[:, 128:256], in0=d_[:, 0:128], in1=Ab[:], op=ALU.mult), r=[d_, Ab], w=[d_])
                    dma("sp", dtda_s[c * 128:(c + 1) * 128, :], d_[:], r=[d_])
            if ODD_STOP == 2:
                return
            for d in range(2):
                if ODD_STOP == 3 and d == 1:
                    return
                with Phase(kb) as P:
                    Sst = P.sb([128, 4096], F32, "Sst")
                    Sb = P.sb([128, 4096], BF16, "Sb")
                    xs = P.sb([128, 4096], F32, "xs")
                    bcm = P.sb([128, 2048], F32, "bcm")
                    dd = P.sb([128, 256], F32, "dd")
                    xsb = P.sb([128, 4096], BF16, "xsb")
                    xsc = P.sb([128, 4096], BF16, "xsc")
                    bcb = P.sb([128, 2048], BF16, "bcb")
                    BT = P.sb([128, 8, 128], BF16, "BT")
                    CTt = P.sb([128, 8, 128], BF16, "CTt")
                    cbT = P.sb([128, 8, 128], F32, "cbT")
                    yacc = P.sb([128, 4096], F32, "yacc")
                    acum = P.sb([128, 64], F32, "acum")
                    nacum = P.sb([128, 64], F32, "nacum")
                    acT = P.sb([64, 128], F32, "acT")
                    acTh = P.sb([64, 128], BF16, "acTh")
                    acThf = P.sb([64, 128], F32, "acThf")
                    acTl = P.sb([64, 128], BF16, "acTl")
                    dah = P.sb([128, 64], BF16, "dah")
                    dahf = P.sb([128, 64], F32, "dahf")
                    dal = P.sb([128, 64], BF16, "dal")
                    aend = P.sb([128, 64], F32, "aend")
                    ea = P.sb([128, 64], F32, "ea")
                    dec = P.sb([128, 64], F32, "dec")
                    te = P.sb([128, 64], F32, "te")
                    Et = [P.sb([128, 128], F32, "Et") for _ in range(2)]
                    WT = [P.sb([128, 128], BF16, "WT") for _ in range(2)]
                    ld = P.sb([128, 4, 128], F32, "ld")
                    pA = P.ps([128, 512], F32, "pA")
                    pT = [P.ps([128, 8, 128], BF16, "pT") for _ in range(2)]
                    pcb = [P.ps([128, 512], F32, "pcb") for _ in range(2)]
                    pa = [P.ps([128, 512], F32, "pa") for _ in range(2)]
                    py = P.ps([128, 512], F32, "py")
                    if d == 1:
                        yf = P.sb([128, 4096], F32, "yf")
                        zb = P.sb([128, 4096], F32, "zb")
                        nrm = load_bc(P, W["ssd_norm"], li * DIN, DIN, "nrm")
                        dsk = load_bc(P, W["d_skip"], li * 64, 64, "dsk")
                        ssy = P.sb([128, 1], F32, "ssy")
                    dma("sp", Sst[:], inits_in[li, d], w=[Sst])
                    op("pool", lambda e: e.tensor_copy(out=Sb[:], in_=Sst[:]), r=[Sst], w=[Sb])
                    order = list(range(NCH)) if d == 0 else list(range(NCH - 1, -1, -1))
                    for idx, c in enumerate(order):
                        if ODD_STOP in (4, 5):
                            break
                        segstart = (c % 2 == 0) if d == 0 else (c % 2 == 1)
                        segend = not segstart
                        sg = c // 2
                        rs = slice(c * 128, (c + 1) * 128)
                        dma("sp", xs[:], xbc_s[rs, 0:4096], w=[xs])
                        dma("act", bcm[:], xbc_s[rs, 4096:6144], w=[bcm])
                        dma("sp", dd[:], dtda_s[rs, :], w=[dd])
                        dt = dd[:, d * 64:(d + 1) * 64]
                        da = dd[:, 128 + d * 64:128 + (d + 1) * 64]
                        op("dve", lambda e: e.tensor_copy(out=dah[:], in_=da), r=[dd], w=[dah])
                        op("dve", lambda e: e.tensor_copy(out=dahf[:], in_=dah[:]), r=[dah], w=[dahf])
                        op("dve", lambda e: e.tensor_tensor(out=dal[:], in0=da, in1=dahf[:], op=ALU.subtract), r=[dd, dahf], w=[dal])
                        op("pe", lambda e: e.matmul(pA[:, 0:64], lhsT=trib[:, d, :], rhs=dah[:], start=True, stop=False), r=[trib, dah], w=[pA], inc=False)
                        op("pe", lambda e: e.matmul(pA[:, 0:64], lhsT=trib[:, d, :], rhs=dal[:], start=False, stop=True), r=[trib, dal], w=[pA], inc=False)
                        op("pe", lambda e: e.matmul(pA[:, 64:128], lhsT=onesb[:], rhs=dah[:], start=True, stop=False), r=[onesb, dah], w=[pA], inc=False)
                        op("pe", lambda e: e.matmul(pA[:, 64:128], lhsT=onesb[:], rhs=dal[:], start=False, stop=True), r=[onesb, dal], w=[pA], inc=False)
                        op("pe", lambda e: e.matmul(pA[0:64, 128:256], lhsT=dah[:], rhs=trib[:, d, :], start=True, stop=False), r=[trib, dah], w=[pA], inc=False)
                        op("pe", lambda e: e.matmul(pA[0:64, 128:256], lhsT=dal[:], rhs=trib[:, d, :], start=False, stop=True), r=[trib, dal], w=[pA])
                        if ODD_STOP == 10:
                            return
                        op("act", lambda e: e.copy(out=acum[:], in_=pA[:, 0:64]), r=[pA], w=[acum])
                        op("dve", lambda e: e.tensor_scalar_mul(out=nacum[:], in0=pA[:, 0:64], scalar1=-1.0), r=[pA], w=[nacum])
                        op("act", lambda e: e.copy(out=aend[:], in_=pA[:, 64:128]), r=[pA], w=[aend])
                        op("dve", lambda e: e.tensor_copy(out=acT[:], in_=pA[0:64, 128:256]), r=[pA], w=[acT])
                        op("dve", lambda e: e.tensor_copy(out=acTh[:], in_=acT[:]), r=[acT], w=[acTh])
                        op("dve", lambda e: e.tensor_copy(out=acThf[:], in_=acTh[:]), r=[acTh], w=[acThf])
                        op("dve", lambda e: e.tensor_tensor(out=acTl[:], in0=acT[:], in1=acThf[:], op=ALU.subtract), r=[acT, acThf], w=[acTl])
                        op("act", lambda e: e.activation(out=ea[:], in_=acum[:], func=AF.Exp), r=[acum], w=[ea])
                        op("act", lambda e: e.activation(out=dec[:], in_=aend[:], func=AF.Exp), r=[aend], w=[dec])
                        op("dve", lambda e: e.tensor_tensor(out=te[:], in0=aend[:], in1=acum[:], op=ALU.subtract), r=[aend, acum], w=[te])
                        op("act", lambda e: e.activation(out=te[:], in_=te[:], func=AF.Exp), r=[te], w=[te])
                        op("dve", lambda e: e.tensor_tensor(out=te[:], in0=te[:], in1=dt, op=ALU.mult), r=[te, dd], w=[te])
                        if ODD_STOP == 11:
                            return
                        if idx > 0 and segstart:
                            op("dve", lambda e: e.tensor_scalar(out=Sst[:], in0=Sst[:], scalar1=flag[:, 0:1], scalar2=None, op0=ALU.mult),
                               r=[Sst, flag], w=[Sst])
                            op("pool", lambda e: e.tensor_copy(out=Sb[:], in_=Sst[:]), r=[Sst], w=[Sb])
                        op("pool", lambda e: e.tensor_copy(out=bcb[:], in_=bcm[:]), r=[bcm], w=[bcb])
                        op("pool", lambda e: e.tensor_copy(out=xsb[:], in_=xs[:]), r=[xs], w=[xsb])
                        for g in range(8):
                            op("pe", lambda e, g=g: e.transpose(pT[0][:, g, :], bcb[:, g * 128:(g + 1) * 128], identb[:]), r=[bcb, identb], w=[pT[0]],
                               inc=(g == 7))
                        op("act", lambda e: e.copy(out=BT[:], in_=pT[0][:]), r=[pT[0]], w=[BT])
                        for g in range(8):
                            op("pe", lambda e, g=g: e.transpose(pT[1][:, g, :], bcb[:, 1024 + g * 128:1024 + (g + 1) * 128], identb[:]), r=[bcb, identb],
                               w=[pT[1]], inc=(g == 7))
                        op("dve", lambda e: e.tensor_copy(out=CTt[:], in_=pT[1][:]), r=[pT[1]], w=[CTt])
                        if ODD_STOP == 12:
                            return
                        for g in range(8):
                            op("pe", lambda e, g=g: e.matmul(pcb[g // 4][:, (g % 4) * 128:(g % 4 + 1) * 128], lhsT=BT[:, g, :], rhs=CTt[:, g, :],
                                                             start=True, stop=True), r=[BT, CTt], w=[pcb[g // 4]], inc=(g % 4 == 3))
                        op("dve", lambda e: e.tensor_tensor(out=cbT[:, 0:4, :], in0=pcb[0][:, 0:512].rearrange("p (g i) -> p g i", g=4),
                                                            in1=bc_mid(trif[:, d, :], 4), op=ALU.mult), r=[pcb[0], trif], w=[cbT])
                        op("dve", lambda e: e.tensor_tensor(out=cbT[:, 4:8, :], in0=pcb[1][:, 0:512].rearrange("p (g i) -> p g i", g=4),
                                                            in1=bc_mid(trif[:, d, :], 4), op=ALU.mult), r=[pcb[1], trif], w=[cbT])
                        if ODD_STOP == 13:
                            return
                        for g in range(8):
                            pp = pcb[g % 2]
                            op("pe", lambda e: e.matmul(pp[:, 0:512], lhsT=CTt[:, g, :], rhs=Sb[:, g * 512:(g + 1) * 512], start=True, stop=True),
                               r=[CTt, Sb], w=[pp])
                            op("dve", lambda e: e.tensor_tensor(out=yacc[:, g * 512:(g + 1) * 512].rearrange("p (h q) -> p h q", h=8),
                                                                in0=pp[:, 0:512].rearrange("p (h q) -> p h q", h=8),
                                                                in1=bc_last(ea[:, g * 8:(g + 1) * 8], 64), op=ALU.mult), r=[pp, ea], w=[yacc])
                        if ODD_STOP == 14:
                            return
                        def issue_A(h):
                            p_a = pa[h % 2]
                            op("pe", lambda e: e.matmul(p_a[:, 0:128], lhsT=identb[0:64, h:h + 1].broadcast_to([64, 128]), rhs=acTh[:, :],
                                                        start=True, stop=False), r=[identb, acTh], w=[p_a], inc=False)
                            op("pe", lambda e: e.matmul(p_a[:, 0:128], lhsT=identb[0:64, h:h + 1].broadcast_to([64, 128]), rhs=acTl[:, :],
                                                        start=False, stop=True), r=[identb, acTl], w=[p_a])
                        issue_A(0)
                        for h in range(64):
                            g, hh = divmod(h, 8)
                            p_a = pa[h % 2]
                            et = Et[h % 2]
                            wt = WT[h % 2]
                            if h + 1 < 64:
                                issue_A(h + 1)
                            op("act", lambda e: e.activation(out=et[:], in_=p_a[:, 0:128], func=AF.Abs, bias=nacum[:, h:h + 1], scale=1.0),
                               r=[p_a, nacum], w=[et])
                            op("act", lambda e: e.activation(out=et[:], in_=et[:], func=AF.Exp, scale=-1.0), r=[et], w=[et])
                            op("dve", lambda e: e.scalar_tensor_tensor(out=wt[:], in0=et[:], scalar=dd[:, d * 64 + h:d * 64 + h + 1], in1=cbT[:, g, :],
                                                                       op0=ALU.mult, op1=ALU.mult), r=[et, dd, cbT], w=[wt])
                            op("pe", lambda e: e.matmul(py[:, hh * 64:(hh + 1) * 64], lhsT=wt[:], rhs=xsb[:, h * 64:(h + 1) * 64], start=True, stop=True),
                               r=[wt, xsb], w=[py])
                            if hh == 7:
                                op("dve", lambda e: e.tensor_tensor(out=yacc[:, g * 512:(g + 1) * 512], in0=yacc[:, g * 512:(g + 1) * 512], in1=py[:, 0:512],
                                                                    op=ALU.add), r=[yacc, py], w=[yacc])
                        if ODD_STOP == 15:
                            return
                        op("dve", lambda e: e.tensor_tensor(out=xsc[:].rearrange("p (h q) -> p h q", h=64), in0=xs[:].rearrange("p (h q) -> p h q", h=64),
                                                            in1=bc_last(te[:, :], 64), op=ALU.mult), r=[xs, te], w=[xsc])
                        for g in range(8):
                            pp = pcb[g % 2]
                            op("pe", lambda e: e.matmul(pp[:, 0:512], lhsT=bcb[:, g * 128:(g + 1) * 128], rhs=xsc[:, g * 512:(g + 1) * 512], start=True, stop=True),
                               r=[bcb, xsc], w=[pp])
                            sv = Sst[:, g * 512:(g + 1) * 512].rearrange("p (h q) -> p h q", h=8)
                            op("dve", lambda e: e.tensor_tensor(out=sv, in0=sv, in1=bc_last(dec[:, g * 8:(g + 1) * 8], 64), op=ALU.mult), r=[Sst, dec], w=[Sst])
                            op("dve", lambda e: e.tensor_tensor(out=Sst[:, g * 512:(g + 1) * 512], in0=Sst[:, g * 512:(g + 1) * 512], in1=pp[:, 0:512], op=ALU.add),
                               r=[Sst, pp], w=[Sst])
                        op("pool", lambda e: e.tensor_copy(out=Sb[:], in_=Sst[:]), r=[Sst], w=[Sb])
                        if ODD_STOP == 16:
                            return
                        if segend:
                            dma("sp", ssd_out[li, sg, d], Sst[:], r=[Sst])
                        if d == 0:
                            dma("sp", hf_s[rs, 0:4096], yacc[:], r=[yacc])
                        else:
                            dma("act", yf[:], hf_s[rs, 0:4096], w=[yf])
                            dma("act", zb[:], proj_s[2 + c * 128:2 + (c + 1) * 128, 0:4096], w=[zb])
                            op("dve", lambda e: e.tensor_tensor(out=yacc[:], in0=yacc[:], in1=yf[:], op=ALU.add), r=[yacc, yf], w=[yacc])
                            op("pool", lambda e: e.tensor_tensor(out=yf[:].rearrange("p (h q) -> p h q", h=64), in0=xs[:].rearrange("p (h q) -> p h q", h=64),
                                                                 in1=bc_last(dsk[:, :], 64), op=ALU.mult), r=[xs, dsk], w=[yf])
                            op("dve", lambda e: e.tensor_tensor(out=yacc[:], in0=yacc[:], in1=yf[:], op=ALU.add), r=[yacc, yf], w=[yacc])
                            op("act", lambda e: e.activation(out=zb[:], in_=zb[:], func=AF.Silu), r=[zb], w=[zb])
                            op("dve", lambda e: e.tensor_tensor(out=yacc[:], in0=yacc[:], in1=zb[:], op=ALU.mult), r=[yacc, zb], w=[yacc])
                            op("act", lambda e: e.activation(out=zb[:], in_=yacc[:], func=AF.Square, accum_out=ssy[:]), r=[yacc], w=[zb, ssy])
                            rstd_inplace(ssy, DIN)
                            op("dve", lambda e: e.scalar_tensor_tensor(out=yacc[:], in0=yacc[:], scalar=ssy[:, 0:1], in1=nrm[:], op0=ALU.mult, op1=ALU.mult),
                               r=[yacc, ssy, nrm], w=[yacc])
                            dma("sp", cat_s[rs, 0:4096], yacc[:], r=[yacc])
        xsrc = x_in
        for l in layers:
            even = (l % 2 == 0)
            li = l // 2
            Win = W["w_in_even"][li] if even else W["w_in_odd"][li]
            EI = EIN if even else OIN

            for ti in range(NTILE):
                with Phase(kb) as P:
                    hT = P.sb([128, 16, TT], BF16, "hT")
                    tp = [P.ps([128, 8, 128], BF16, "tp") for _ in range(2)]
                    pls = [P.ps([128, 512], F32, "pl") for _ in range(4)]
                    wbufs = [P.sb([128, 16, 512], BF16, "wb") for _ in range(3)]
                    obs = [P.sb([128, 512], F32, "ob") for _ in range(3)]
                    with Phase(kb) as P2:
                        xts = [P2.sb([128, D], F32, "xt") for _ in range(2)]
                        run = make_norm(P2, l, 0, hT, tp)
                        for tc in range(NTC):
                            xt = xts[tc % 2]
                            r0 = ti * TT + tc * 128
                            dma("sp", xt[:], xsrc[r0:r0 + 128, :], w=[xt])
                            run(tc, xt, xt[:])
                    oc = [0]

                    def epi(tc, e0, ec, ps):
                        ob = obs[oc[0] % 3]
                        oc[0] += 1
                        if oc[0] % 2 == 0:
                            op("act", lambda e: e.copy(out=ob[:, 0:ec], in_=ps[:, 0:ec]), r=[ps], w=[ob])
                        else:
                            op("dve", lambda e: e.tensor_copy(out=ob[:, 0:ec], in_=ps[:, 0:ec]), r=[ps], w=[ob])
                        r0 = 2 + ti * TT + tc * 128
                        dma("sp", proj_s[r0:r0 + 128, e0:e0 + ec], ob[:, 0:ec], r=[ob])
                    linear(hT, NTC, 16, Win, EI, wbufs, pls, epi)

            if even:
                even_mixer(l, li)
            else:
                odd_mixer(l, li)
            KC = D if even else DIN
            Wout = W["w_out_even"][li] if even else W["w_out_odd"][li]

            for ti in range(NTILE):
                with Phase(kb) as P:
                    nk = KC // 128
                    ebw = 512 if nk == 16 else 256
                    catT = P.sb([128, nk, TT], BF16, "catT")
                    tp = [P.ps([128, 8, 128], BF16, "tp") for _ in range(2)]
                    pls = [P.ps([128, 512], F32, "pl") for _ in range(4)]
                    wbufs = [P.sb([128, nk, ebw], BF16, "wb") for _ in range(3)]
                    xres = P.sb([128, NTC, D], F32, "xres")
                    g1 = load_bc(P, mod_s, l * 6 * D + 2 * D, D, "g1")
                    tmps = [P.sb([128, 512], F32, "tmp") for _ in range(2)]
                    with Phase(kb) as P2:
                        cf = [P2.sb([128, KC], F32, "cf") for _ in range(2)]
                        cbb = [P2.sb([128, KC], BF16, "cbb") for _ in range(2)]
                        for tc in range(NTC):
                            r0 = ti * TT + tc * 128
                            dma("sp", xres[:, tc, :], xsrc[r0:r0 + 128, :], w=[xres])
                            c_f = cf[tc % 2]
                            c_b = cbb[tc % 2]
                            dma("act", c_f[:], cat_s[r0:r0 + 128, 0:KC], w=[c_f])
                            op("pool", lambda e: e.tensor_copy(out=c_b[:], in_=c_f[:]), r=[c_f], w=[c_b])
                            transpose_to(c_b, KC, catT, tc * 128, tp)
                    oc = [0]

                    def epi(tc, e0, ec, ps):
                        t_ = tmps[oc[0] % 2]
                        oc[0] += 1
                        op("dve", lambda e: e.tensor_tensor(out=t_[:, 0:ec], in0=ps[:, 0:ec], in1=g1[:, e0:e0 + ec], op=ALU.mult),
                           r=[ps, g1], w=[t_])
                        op("dve", lambda e: e.tensor_tensor(out=xres[:, tc, e0:e0 + ec], in0=xres[:, tc, e0:e0 + ec], in1=t_[:, 0:ec],
                                                           op=ALU.add), r=[xres, t_], w=[xres])
                    linear(catT, NTC, nk, Wout, D, wbufs, pls, epi, ebw=ebw)
                    for tc in range(NTC):
                        r0 = ti * TT + tc * 128
                        dma("sp", y_out[r0:r0 + 128, :], xres[:, tc, :], r=[xres])
            xsrc = y_out

            for ti in range(NTILE):
                with Phase(kb) as P:
                    hT = P.sb([128, 16, TT], BF16, "hT")
                    ffT = P.sb([128, 44, TT], BF16, "ffT")
                    tp = [P.ps([128, 8, 128], BF16, "tp") for _ in range(2)]
                    pls = [P.ps([128, 512], F32, "pl") for _ in range(4)]
                    xres = P.sb([128, NTC, D], F32, "xres")
                    with Phase(kb) as P2:
                        run = make_norm(P2, l, 1, hT, tp)
                        for tc in range(NTC):
                            r0 = ti * TT + tc * 128
                            dma("sp", xres[:, tc, :], xsrc[r0:r0 + 128, :], w=[xres])
                            run(tc, xres, xres[:, tc, :])
                    with Phase(kb) as P3:
                        wg = [P3.sb([128, 16, 512], BF16, "wg") for _ in range(2)]
                        wu = [P3.sb([128, 16, 512], BF16, "wu") for _ in range(2)]
                        sgs = [P3.sb([128, 512], F32, "sg") for _ in range(2)]
                        ffb = [P3.sb([128, 512], BF16, "ffb") for _ in range(2)]
                        Wg = W["w_gate"][l].rearrange("(k p) e -> p k e", p=128)
                        Wu = W["w_up"][l].rearrange("(k p) e -> p k e", p=128)
                        cc = 0
                        pend_t = [None]
                        for eb in range(11):
                            e0 = eb * 512
                            wbg = wg[eb % 2]
                            wbu = wu[eb % 2]
                            dma("pool", wbg[:], Wg[:, :, e0:e0 + 512], w=[wbg])
                            dma("pool", wbu[:], Wu[:, :, e0:e0 + 512], w=[wbu])
                            for tc in range(NTC):
                                pg = pls[(cc * 2) % 4]
                                pu = pls[(cc * 2 + 1) % 4]
                                sg = sgs[cc % 2]
                                fb = ffb[cc % 2]
                                pt = tp[cc % 2]
                                cc += 1
                                for k in range(16):
                                    op("pe", lambda e, k=k: e.matmul(pg[:], lhsT=hT[:, k, tc * 128:(tc + 1) * 128], rhs=wbg[:, k, :],
                                                                     start=(k == 0), stop=(k == 15)), r=[hT, wbg], w=[pg], inc=(k == 15))
                                for k in range(16):
                                    op("pe", lambda e, k=k: e.matmul(pu[:], lhsT=hT[:, k, tc * 128:(tc + 1) * 128], rhs=wbu[:, k, :],
                                                                     start=(k == 0), stop=(k == 15)), r=[hT, wbu], w=[pu], inc=(k == 15))
                                if pend_t[0] is not None:
                                    pend_t[0]()
                                    pend_t[0] = None
                                op("act", lambda e: e.activation(out=sg[:], in_=pg[:], func=AF.Silu), r=[pg], w=[sg])
                                op("dve", lambda e: e.tensor_tensor(out=fb[:], in0=sg[:], in1=pu[:], op=ALU.mult), r=[sg, pu], w=[fb])

                                def do_tr(fb=fb, pt=pt, eb=eb, tc=tc):
                                    for k in range(4):
                                        op("pe", lambda e, k=k: e.transpose(pt[:, k, :], fb[:, k * 128:(k + 1) * 128], identb[:]),
                                           r=[fb, identb], w=[pt], inc=(k == 3))
                                    op("act", lambda e: e.copy(out=ffT[:, eb * 4:eb * 4 + 4, tc * 128:(tc + 1) * 128], in_=pt[:, 0:4, :]),
                                       r=[pt], w=[ffT])
                                pend_t[0] = do_tr
                        if pend_t[0] is not None:
                            pend_t[0]()
                            pend_t[0] = None
                    with Phase(kb) as P4:
                        g2 = load_bc(P4, mod_s, l * 6 * D + 5 * D, D, "g2")
                        tmps = [P4.sb([128, 256], F32, "tmp") for _ in range(2)]
                        wd = [P4.sb([128, 44, 256], BF16, "wd") for _ in range(3)]
                        oc = [0]

                        def epi(tc, e0, ec, ps):
                            t_ = tmps[oc[0] % 2]
                            oc[0] += 1
                            op("dve", lambda e: e.tensor_tensor(out=t_[:, 0:ec], in0=ps[:, 0:ec], in1=g2[:, e0:e0 + ec], op=ALU.mult),
                               r=[ps, g2], w=[t_])
                            op("dve", lambda e: e.tensor_tensor(out=xres[:, tc, e0:e0 + ec], in0=xres[:, tc, e0:e0 + ec], in1=t_[:, 0:ec],
                                                               op=ALU.add), r=[xres, t_], w=[xres])
                        linear(ffT, NTC, 44, W["w_down"][l], D, wd, pls, epi, ebw=256)
                        for tc in range(NTC):
                            r0 = ti * TT + tc * 128
                            dma("sp", y_out[r0:r0 + 128, :], xres[:, tc, :], r=[xres])
        G.__exit__(None, None, None)
    return nc, I, O


WSHAPES = [("w_ada", [4, D, 6 * D]), ("b_ada", [4, 6 * D]), ("norm_mix", [4, D]), ("norm_ffn", [4, D]),
           ("w_gate", [4, D, DFF]), ("w_up", [4, D, DFF]), ("w_down", [4, DFF, D]),
           ("w_in_even", [2, D, EIN]), ("b_gate_mlstm", [2, 32]), ("mlstm_norm", [2, 1024]),
           ("q_norm", [2, 64]), ("k_norm", [2, 64]), ("lambda_q1", [2, 64]), ("lambda_k1", [2, 64]),
           ("lambda_q2", [2, 64]), ("lambda_k2", [2, 64]), ("diff_norm", [2, 128]),
           ("w_out_even", [2, D, D]), ("w_in_odd", [2, D, OIN]), ("conv_w", [2, 4, CONVC]),
           ("conv_b", [2, CONVC]), ("dt_bias", [2, 2, 64]), ("a_log", [2, 2, 64]), ("d_skip", [2, 64]),
           ("ssd_norm", [2, DIN]), ("w_out_odd", [2, DIN, D])]


def core_inputs(NSEG, kind, xtok, cvec, weights, ctxk=None, ctxv=None, initC=None, initn=None, initm=None, inits=None):
    NT = NSEG * 256
    NK = 256 + NT
    f = 1.0 if kind == "sample" else 0.0
    bf = ml_dtypes.bfloat16
    m = {}
    m["x"] = np.ascontiguousarray(xtok, dtype=np.float32)
    m["cT"] = np.ascontiguousarray(cvec.reshape(16, 128).T, dtype=np.float32)
    m["flag"] = np.full((128, 1), f, np.float32)
    cm = np.ones((128, 3), np.float32)
    cm[0:2, 0] = f
    cm[0:1, 1] = f
    cm[127, 2] = f
    m["cmask"] = cm
    m["identf"] = np.eye(128, dtype=np.float32)
    m["identb"] = np.eye(128, dtype=np.float32).astype(bf)
    t = np.arange(128)
    tri = np.stack([(t[:, None] <= t[None, :]), (t[:, None] >= t[None, :])]).astype(np.float32)
    m["trif"] = tri
    m["negm"] = ((1.0 - tri) * (-BIG)).astype(bf)
    if kind == "sample":
        pos = np.arange(NT)
        rows = (pos // 64).astype(np.float32)
        cols = (pos % 64).astype(np.float32)
        inv = (10000.0 ** (-np.arange(16, dtype=np.float32) / 16.0)).astype(np.float32)
        ang = np.concatenate([rows[:, None] * inv[None, :], cols[:, None] * inv[None, :]], axis=1).astype(np.float32)
        m["cosT"] = np.cos(ang).astype(np.float32)
        m["sinT"] = np.sin(ang).astype(np.float32)
        m["qseg"] = np.zeros((16, NT), bf)
    else:
        m["cosT"] = np.ones((NT, 32), np.float32)
        m["sinT"] = np.zeros((NT, 32), np.float32)
        q = np.zeros((16, NT), np.float32)
        q[np.arange(NT) // 256, np.arange(NT)] = 1.0
        m["qseg"] = q.astype(bf)
    ks = np.zeros((16, NK), np.float32)
    ks[0:8, 0:256] = -BIG
    sk = np.arange(NT) // 256
    for s in range(NSEG):
        ks[s, 256:] = np.where(sk == s, 0.0, -BIG)
    for s in range(NSEG, 8):
        ks[s, 256:] = 0.0
    m["kseg"] = ks.astype(bf)
    z = lambda *s: np.zeros(s, np.float32)
    m["ctxk"] = z(2, 256, 1024) if ctxk is None else np.ascontiguousarray(ctxk, np.float32)
    m["ctxv"] = z(2, 256, 1024) if ctxv is None else np.ascontiguousarray(ctxv, np.float32)
    m["initC"] = z(2, 2, 8, 128, 64) if initC is None else np.ascontiguousarray(initC, np.float32)
    m["initn"] = z(2, 2, 8, 64) if initn is None else np.ascontiguousarray(initn, np.float32)
    m["initm"] = z(2, 2, 8) if initm is None else np.ascontiguousarray(initm, np.float32)
    m["inits"] = z(2, 2, 128, 4096) if inits is None else np.ascontiguousarray(np.asarray(inits, np.float32).reshape(2, 2, 4096, 128).transpose(0, 1, 3, 2))
    m.update(weights)
    return m


_CACHE = {}


def kernel(**inp):
    NSEG = 8
    inp = {k: np.asarray(v) for k, v in inp.items()}
    weights = {nm: np.ascontiguousarray(inp[nm], dtype=np.float32) for nm, _ in WSHAPES}
    key = (NSEG, (0, 1, 2, 3))
    if key not in _CACHE:
        _CACHE[key] = build(NSEG, [0, 1, 2, 3])
    nc, I, O = _CACHE[key]
    xp = inp["x_prompt"]
    xs = inp["x_sample"]
    counts = [6, 6, 5, 5, 5, 5]
    starts = np.concatenate([[0], np.cumsum(counts)])
    in_maps = []
    for b in range(2):
        in_maps.append(core_inputs(
            NSEG, "sample", xs[b], inp["c"][b], weights,
            ctxk=inp["cache_attn_k"][b].reshape(2, 256, 1024), ctxv=inp["cache_attn_v"][b].reshape(2, 256, 1024),
            initC=inp["state_mlstm_c"][b], initn=inp["state_mlstm_n"][b], initm=inp["state_mlstm_m"][b],
            inits=inp["state_ssd"][b].reshape(2, 2, 4096, 128)))
    for p in range(6):
        xt = np.zeros((NSEG * 256, D), np.float32)
        n = counts[p]
        xt[:n * 256] = xp[starts[p]:starts[p] + n].reshape(n * 256, D)
        in_maps.append(core_inputs(NSEG, "prompt", xt, inp["c_ctx"], weights))
    res = run_bass_kernel_spmd(nc, in_maps, core_ids=list(range(8)))
    R = res.results
    y_sample = np.stack([R[b]["y"] for b in range(2)]).astype(np.float32)
    B = xp.shape[0]
    y_prompt = np.zeros((B, 256, D), np.float32)
    nk = np.zeros((B, 2, 256, 8, 2, 64), np.float32)
    nv = np.zeros((B, 2, 256, 8, 128), np.float32)
    nC = np.zeros((B, 2, 2, 8, 128, 64), np.float32)
    nn = np.zeros((B, 2, 2, 8, 64), np.float32)
    nm_ = np.zeros((B, 2, 2, 8), np.float32)
    ns = np.zeros((B, 2, 2, 64, 64, 128), np.float32)
    for p in range(6):
        r = R[2 + p]
        for s in range(counts[p]):
            q = starts[p] + s
            y_prompt[q] = r["y"][s * 256:(s + 1) * 256]
            nk[q] = r["kd"][:, s * 256:(s + 1) * 256].reshape(2, 256, 8, 2, 64)
            nv[q] = r["vd"][:, s * 256:(s + 1) * 256].reshape(2, 256, 8, 128)
            nC[q] = r["mC"][:, s]
            nn[q] = r["mn"][:, s]
            nm_[q] = r["mm"][:, s]
            ns[q] = r["ssd"][:, s].transpose(0, 1, 3, 2).reshape(2, 2, 64, 64, 128)
    return (y_prompt, y_sample, nk, nv, nC, nn, nm_, ns)
```
